# Optimizing a Trainium2 kernel written in Bass

```python
import math
import jax
import jax.numpy as jnp
from jax import lax
import numpy as np

D_MODEL = 1024
BATCH = 8
SEQ = 2048
DEPTH = 2

HEAD_DIM = 64
D_MIX = D_MODEL
GROUP_W = D_MIX // 4
Q_BLOCK = 128
EPS = 1e-6
DIL_HEADS = GROUP_W // HEAD_DIM
DILATED_PATTERNS = ((128, 1), (512, 4), (2048, 16))
DIFF_HEADS = 4
DIFF_HEAD_DIM = GROUP_W // (2 * DIFF_HEADS)
SSM_HEADS = GROUP_W // HEAD_DIM
SSM_HEAD_DIM = HEAD_DIM
SSM_GROUPS = 2
SSM_STATE = 128
SSM_CONV = 5
SSM_CHUNK = 128
SSM_XBC = GROUP_W + 2 * SSM_GROUPS * SSM_STATE
NA_HEADS = GROUP_W // HEAD_DIM
GRID_W = 64
NA_WIN_H = 8
NA_WIN_W = 16
PROJ_WIDTHS = (GROUP_W, GROUP_W, GROUP_W, GROUP_W,
               GROUP_W, GROUP_W, GROUP_W, GROUP_W,
               GROUP_W, SSM_XBC, 2 * SSM_HEADS,
               GROUP_W, GROUP_W, GROUP_W, GROUP_W)
D_IN = 13 * GROUP_W + SSM_XBC + 2 * SSM_HEADS

kernel_name = 'hybrid_parallel_group_encoder'


def rms_norm(x, w):
    x32 = x.astype(jnp.float32)
    y = x32 * lax.rsqrt(jnp.mean(x32 * x32, axis=-1, keepdims=True) + EPS)
    return (y * w.astype(jnp.float32)).astype(x.dtype)


def split_cols(t, widths):
    outs = []
    start = 0
    for w in widths:
        outs.append(t[..., start:start + w])
        start += w
    return outs


def to_heads(t, n):
    b, s, _ = t.shape
    return t.reshape(b, s, n, -1).transpose(0, 2, 1, 3)


def to_pair_heads(t):
    b, s, _ = t.shape
    return t.reshape(b, s, DIFF_HEADS, 2, DIFF_HEAD_DIM).transpose(0, 2, 3, 1, 4)


def from_heads(t):
    b, h, s, d = t.shape
    return t.transpose(0, 2, 1, 3).reshape(b, s, h * d)


def alibi_slopes():
    n = DIL_HEADS + DIFF_HEADS
    m = 2.0 ** (-8.0 * jnp.arange(1, n + 1, dtype=jnp.float32) / n)
    return m[0::2], m[1::2]


def dilated_attention(q, k, v, slopes):
    T = q.shape[2]
    scale = q.shape[-1] ** -0.5
    qf, kf, vf = q.astype(jnp.float32), k.astype(jnp.float32), v.astype(jnp.float32)
    pats = []
    for w, r in DILATED_PATTERNS:
        half = w // (2 * r)
        offs = r * np.arange(-half, half + 1)
        pats.append((offs, jnp.asarray(np.abs(offs), jnp.float32)))

    def block(blk):
        t0 = blk * Q_BLOCK
        pos = t0 + jnp.arange(Q_BLOCK)
        qb = lax.dynamic_slice_in_dim(qf, t0, Q_BLOCK, axis=2)
        outs, lses = [], []
        for offs, dist in pats:
            idx = pos[:, None] + offs[None, :]
            valid = (idx >= 0) & (idx < T)
            idx = jnp.clip(idx, 0, T - 1)
            kg = kf[:, :, idx]
            vg = vf[:, :, idx]
            s = jnp.einsum('bhqd,bhqkd->bhqk', qb, kg) * scale - slopes[:, None, None] * dist[None, None, :]
            s = jnp.where(valid[None, None], s, -jnp.inf)
            lse = jax.nn.logsumexp(s, axis=-1)
            outs.append(jnp.einsum('bhqk,bhqkd->bhqd', jnp.exp(s - lse[..., None]), vg))
            lses.append(lse)
        alpha = jax.nn.softmax(jnp.stack(lses), axis=0)
        return jnp.einsum('pbhq,pbhqd->bhqd', alpha, jnp.stack(outs))

    out = lax.map(block, jnp.arange(T // Q_BLOCK))
    return out.transpose(1, 2, 0, 3, 4).reshape(q.shape).astype(q.dtype)


def diff_attention(q, k, v, lam, slopes):
    T = v.shape[2]
    scale = q.shape[-1] ** -0.5
    qf, kf, vf = q.astype(jnp.float32), k.astype(jnp.float32), v.astype(jnp.float32)
    key_pos = jnp.arange(T)

    def block(blk):
        t0 = blk * Q_BLOCK
        pos = t0 + jnp.arange(Q_BLOCK)
        qb = lax.dynamic_slice_in_dim(qf, t0, Q_BLOCK, axis=3)
        bias = -slopes[:, None, None] * jnp.abs(pos[:, None] - key_pos[None, :]).astype(jnp.float32)
        s = jnp.einsum('bhiqd,bhikd->bhiqk', qb, kf) * scale + bias[None, :, None]
        a = jax.nn.softmax(s, axis=-1)
        return jnp.einsum('bhqk,bhkd->bhqd', a[:, :, 0] - lam * a[:, :, 1], vf)

    out = lax.map(block, jnp.arange(T // Q_BLOCK))
    return out.transpose(1, 2, 0, 3, 4).reshape(v.shape).astype(v.dtype)


def segsum(a):
    L = a.shape[-1]
    a_rep = jnp.broadcast_to(a[..., :, None], a.shape + (L,))
    a_rep = jnp.where(jnp.tril(jnp.ones((L, L), bool), -1), a_rep, 0.0)
    cs = jnp.cumsum(a_rep, axis=-2)
    return jnp.where(jnp.tril(jnp.ones((L, L), bool)), cs, -jnp.inf)


def ssd_scan(X, A, Bm, Cm):
    b, T, h, p = X.shape
    nc, l = T // SSM_CHUNK, SSM_CHUNK
    X = X.reshape(b, nc, l, h, p)
    Bm = Bm.reshape(b, nc, l, h, -1)
    Cm = Cm.reshape(b, nc, l, h, -1)
    A = A.reshape(b, nc, l, h).transpose(0, 3, 1, 2)
    A_cum = jnp.cumsum(A, axis=-1)
    Lmat = jnp.exp(segsum(A))
    CB = jnp.einsum('bclhn,bcshn->bhcls', Cm, Bm)
    y_diag = jnp.einsum('bhcls,bcshp->bclhp', CB * Lmat, X)
    decay_states = jnp.exp(A_cum[..., -1:] - A_cum)
    states = jnp.einsum('bclhn,bhcl,bclhp->bchpn', Bm, decay_states, X)
    states = jnp.concatenate([jnp.zeros_like(states[:, :1]), states], axis=1)
    decay_chunk = jnp.exp(segsum(jnp.pad(A_cum[..., -1], ((0, 0), (0, 0), (1, 0)))))
    states = jnp.einsum('bhzc,bchpn->bzhpn', decay_chunk, states)[:, :-1]
    y_off = jnp.einsum('bclhn,bchpn,bhcl->bclhp', Cm, states, jnp.exp(A_cum))
    return (y_diag + y_off).reshape(b, T, h, p)


def ssd_mixer(z, xbc, dt_raw, conv_w, conv_b, a_log, dt_bias, d_skip, norm_w):
    b, T, _ = xbc.shape
    xbc = lax.conv_general_dilated(xbc, conv_w[:, None, :].astype(xbc.dtype), window_strides=(1,),
                                   padding=[(SSM_CONV // 2, SSM_CONV // 2)],
                                   dimension_numbers=('NWC', 'WIO', 'NWC'),
                                   feature_group_count=SSM_XBC) + conv_b
    xbc = jax.nn.silu(xbc).astype(jnp.float32)
    xs, Bm, Cm = split_cols(xbc, (GROUP_W, SSM_GROUPS * SSM_STATE, SSM_GROUPS * SSM_STATE))
    rep = SSM_HEADS // SSM_GROUPS
    xs = xs.reshape(b, T, SSM_HEADS, SSM_HEAD_DIM)
    Bm = jnp.repeat(Bm.reshape(b, T, SSM_GROUPS, SSM_STATE), rep, axis=2)
    Cm = jnp.repeat(Cm.reshape(b, T, SSM_GROUPS, SSM_STATE), rep, axis=2)
    dt = jax.nn.softplus(dt_raw.reshape(b, T, 2, SSM_HEADS).astype(jnp.float32) + dt_bias.astype(jnp.float32))
    A = -jnp.exp(a_log.astype(jnp.float32))
    y_f = ssd_scan(xs * dt[:, :, 0, :, None], A[0] * dt[:, :, 0], Bm, Cm)
    flip = lambda t: jnp.flip(t, axis=1)
    y_b = flip(ssd_scan(flip(xs * dt[:, :, 1, :, None]), flip(A[1] * dt[:, :, 1]), flip(Bm), flip(Cm)))
    y = y_f + y_b + xs * d_skip.astype(jnp.float32)[:, None]
    y = y.reshape(b, T, GROUP_W) * jax.nn.silu(z.astype(jnp.float32))
    yg = y.reshape(b, T, SSM_GROUPS, GROUP_W // SSM_GROUPS)
    yg = yg * lax.rsqrt(jnp.mean(yg * yg, axis=-1, keepdims=True) + EPS)
    return (yg.reshape(b, T, GROUP_W) * norm_w.astype(jnp.float32)).astype(z.dtype)


def neighborhood_attention(q, k, v, rpb):
    b, h, T, d = q.shape
    rows = T // GRID_W
    kh, kw = min(NA_WIN_H, rows), NA_WIN_W
    scale = d ** -0.5
    qg = q.reshape(b, h, rows, GRID_W, d).astype(jnp.float32)
    kg = k.reshape(b, h, rows, GRID_W, d).astype(jnp.float32)
    vg = v.reshape(b, h, rows, GRID_W, d).astype(jnp.float32)
    cols = np.arange(GRID_W)
    col_idx = np.clip(cols - kw // 2, 0, GRID_W - kw)[:, None] + np.arange(kw)[None, :]
    dc = col_idx - cols[:, None] + NA_WIN_W - 1
    rpb = rpb.astype(jnp.float32)

    def row(r):
        rs = jnp.clip(r - kh // 2, 0, rows - kh)
        qr = lax.dynamic_index_in_dim(qg, r, axis=2, keepdims=False)
        kr = lax.dynamic_slice_in_dim(kg, rs, kh, axis=2)[:, :, :, col_idx]
        vr = lax.dynamic_slice_in_dim(vg, rs, kh, axis=2)[:, :, :, col_idx]
        dr = rs + jnp.arange(kh) - r + NA_WIN_H - 1
        bias = rpb[:, dr[None, :, None], dc[:, None, :]]
        s = jnp.einsum('bhqd,bhrqkd->bhqrk', qr, kr) * scale + bias[None]
        p = jax.nn.softmax(s.reshape(b, h, GRID_W, kh * kw), axis=-1).reshape(b, h, GRID_W, kh, kw)
        return jnp.einsum('bhqrk,bhrqkd->bhqd', p, vr)

    out = lax.map(row, jnp.arange(rows))
    return out.transpose(1, 2, 0, 3, 4).reshape(b, h, T, d).astype(q.dtype)


def setup_inputs(seed: int = 0) -> dict:
    key = jax.random.key(seed)
    ks = jax.random.split(key, 17)
    f32 = jnp.float32

    def nrm(k, shape, s):
        return s * jax.random.normal(k, shape, f32)

    x = nrm(ks[0], (BATCH, SEQ, D_MODEL), 1.0)
    c = nrm(ks[1], (BATCH, D_MODEL), 1.0)
    norm_w = 1.0 + nrm(ks[2], (DEPTH, D_MODEL), 0.02)
    ada_w = nrm(ks[3], (DEPTH, D_MODEL, 3 * D_MODEL), D_MODEL ** -0.5)
    ada_b = nrm(ks[4], (DEPTH, 3 * D_MODEL), 0.02)
    w_in = nrm(ks[5], (DEPTH, D_MODEL, D_IN), D_MODEL ** -0.5)
    diff_lambda = nrm(ks[6], (DEPTH, 4, DIFF_HEAD_DIM), 0.1)
    diff_norm_w = 1.0 + nrm(ks[7], (DEPTH, 2 * DIFF_HEAD_DIM), 0.02)
    conv_w = nrm(ks[8], (DEPTH, SSM_CONV, SSM_XBC), SSM_CONV ** -0.5)
    conv_b = nrm(ks[9], (DEPTH, SSM_XBC), 0.02)
    ssm_a_log = jnp.log(jax.random.uniform(ks[10], (DEPTH, 2, SSM_HEADS), f32, 1.0, 16.0))
    dt0 = jnp.exp(jax.random.uniform(ks[11], (DEPTH, 2, SSM_HEADS), f32, math.log(1e-3), math.log(1e-1)))
    ssm_dt_bias = dt0 + jnp.log(-jnp.expm1(-dt0))
    ssm_d = 1.0 + nrm(ks[12], (DEPTH, SSM_HEADS), 0.1)
    ssm_norm_w = 1.0 + nrm(ks[13], (DEPTH, GROUP_W), 0.02)
    na_rpb = nrm(ks[14], (DEPTH, NA_HEADS, 2 * NA_WIN_H - 1, 2 * NA_WIN_W - 1), 0.02)
    w_out = nrm(ks[15], (DEPTH, D_MIX, D_MODEL), D_MIX ** -0.5)
    final_norm_w = 1.0 + nrm(ks[16], (D_MODEL,), 0.02)
    return {'x': x, 'c': c, 'norm_w': norm_w, 'ada_w': ada_w, 'ada_b': ada_b, 'w_in': w_in,
            'diff_lambda': diff_lambda, 'diff_norm_w': diff_norm_w, 'conv_w': conv_w, 'conv_b': conv_b,
            'ssm_a_log': ssm_a_log, 'ssm_dt_bias': ssm_dt_bias, 'ssm_d': ssm_d, 'ssm_norm_w': ssm_norm_w,
            'na_rpb': na_rpb, 'w_out': w_out, 'final_norm_w': final_norm_w}


def reference(x, c, norm_w, ada_w, ada_b, w_in, diff_lambda, diff_norm_w, conv_w, conv_b,
              ssm_a_log, ssm_dt_bias, ssm_d, ssm_norm_w, na_rpb, w_out, final_norm_w):
    slopes_a, slopes_b = alibi_slopes()
    c_act = jax.nn.silu(c)
    for l in range(DEPTH):
        mod = c_act @ ada_w[l] + ada_b[l]
        shift, scale, gate = jnp.split(mod[:, None, :], 3, axis=-1)
        h = rms_norm(x, norm_w[l]) * (1.0 + scale) + shift
        proj = h @ w_in[l]
        (aq, ak, av, ag, bq, bk, bv, bg, cz, cxbc, cdt, dq, dk, dv, dg) = split_cols(proj, PROJ_WIDTHS)
        o_a = dilated_attention(to_heads(aq, DIL_HEADS), to_heads(ak, DIL_HEADS), to_heads(av, DIL_HEADS), slopes_a)
        y_a = from_heads(o_a) * jax.nn.silu(ag)
        lam_init = 0.8 - 0.6 * math.exp(-0.3 * l)
        lv = diff_lambda[l].astype(jnp.float32)
        lam = jnp.exp(jnp.sum(lv[0] * lv[1])) - jnp.exp(jnp.sum(lv[2] * lv[3])) + lam_init
        o_b = diff_attention(to_pair_heads(bq), to_pair_heads(bk), to_heads(bv, DIFF_HEADS), lam, slopes_b)
        o_b = rms_norm(o_b, diff_norm_w[l]) * (1.0 - lam_init)
        y_b = from_heads(o_b) * jax.nn.silu(bg)
        y_c = ssd_mixer(cz, cxbc, cdt, conv_w[l], conv_b[l], ssm_a_log[l], ssm_dt_bias[l], ssm_d[l], ssm_norm_w[l])
        o_d = neighborhood_attention(to_heads(dq, NA_HEADS), to_heads(dk, NA_HEADS), to_heads(dv, NA_HEADS), na_rpb[l])
        y_d = from_heads(o_d) * jax.nn.silu(dg)
        y = jnp.concatenate([y_a, y_b, y_c, y_d], axis=-1) @ w_out[l]
        x = x + gate * y
    return rms_norm(x, final_norm_w)
```

```python
import math
import numpy as np
import concourse.bass as bass
import concourse.mybir as mybir
from concourse.bass_utils import run_bass_kernel_spmd
from contextlib import ExitStack

F32 = mybir.dt.float32
BF16 = mybir.dt.bfloat16
I16 = mybir.dt.int16
AF = mybir.ActivationFunctionType
ALU = mybir.AluOpType
AX = mybir.AxisListType

T = 2048
D = 1024
NB = 16
DIN = 4104
EPS = 1e-6
TW = 3968
C0 = 1920
NROW = 8 + 8 + 4 + 256 + 256


class _Own:
    def __init__(self, sem, name):
        self.sem = sem
        self.count = 0
        self.name = name
        self.seen = {}
        self.last = None
        self.h = None


class KB:
    def __init__(self, nc, es):
        self.nc = nc
        self.es = es
        self.eng = {}
        for name, h in (("pe", nc.tensor), ("act", nc.scalar), ("dve", nc.vector),
                        ("pool", nc.gpsimd), ("sp", nc.sync)):
            o = _Own(es.enter_context(nc.semaphore("p_" + name)), name)
            o.h = h
            self.eng[name] = o
        self.dsems = [_Own(es.enter_context(nc.semaphore("dq%d" % i)), "dq%d" % i) for i in range(48)]
        self.di = 0
        self.bufs = {}
        self.stopped = False
        self.consumed = set()

    def _deps(self, reads, writes):
        deps = []
        for b in reads:
            st = self.bufs.get(b)
            if st and st[0] is not None:
                deps.append(st[0])
        for b in writes:
            st = self.bufs.get(b)
            if st:
                if st[0] is not None:
                    deps.append(st[0])
                deps.extend(st[1])
        return deps

    def _wait(self, e, tok):
        own, val = tok
        if e.seen.get(own, 0) >= val:
            return
        e.h.wait_ge(own.sem, val)
        e.seen[own] = val
        self.consumed.add((id(own), val))

    def _record(self, tok, reads, writes):
        for b in reads:
            st = self.bufs.setdefault(b, [None, []])
            st[1].append(tok)
        for b in writes:
            self.bufs[b] = [tok, []]

    def barrier(self):
        if self.stopped:
            return
        toks = [(e, e.count) for e in self.eng.values() if e.count > 0]
        toks += [d.last for d in self.dsems if d.last is not None and (id(d.last[0]), d.last[1]) not in self.consumed]
        for e in self.eng.values():
            for tok in toks:
                if tok[0] is not e:
                    self._wait(e, tok)

    def op(self, en, fn, r=(), w=(), sig=True):
        if self.stopped:
            return None
        e = self.eng[en]
        pr = [k for k in r if isinstance(k, tuple) and k[0] == "ps"]
        if pr:
            w = list(w) + [k for k in pr if k not in w]
            r = [k for k in r if k not in pr]
        for tok in self._deps(r, w):
            if tok[0] is e and en == "pe":
                continue
            self._wait(e, tok)
        inst = fn()
        if sig:
            e.count += 1
            inst.then_inc(e.sem, 1)
            tok = (e, e.count)
        else:
            tok = (e, e.count + 1)
        self._record(tok, r, w)
        return tok

    def dma(self, qn, out, in_, r=(), w=(), **kw):
        if self.stopped:
            return None
        e = self.eng[qn]
        d = self.dsems[self.di % len(self.dsems)]
        self.di += 1
        if d.last is not None:
            self._wait(e, d.last)
        for tok in self._deps(r, w):
            self._wait(e, tok)
        inst = e.h.dma_start(out=out, in_=in_, **kw)
        d.count += 16
        inst.then_inc(d.sem, 16)
        tok = (d, d.count)
        d.last = tok
        self._record(tok, r, w)
        return tok


class _Stop(Exception):
    pass


def build_program(dbg=False, stop=None):
    nc = bass.Bass("TRN2", target_bir_lowering=False)
    dr = {}

    def din(name, shape):
        dr[name] = nc.dram_tensor(name, list(shape), F32, kind="ExternalInput")
        return dr[name].ap()

    x_d = din("x", [T, D])
    ccol_d = din("ccol", [128, 8])
    cst_d = din("cst", [128, 770])
    adaw_d = din("ada_w", [2, 24, 128, 1024])
    adab_d = din("ada_b", [1, 2 * 3 * D])
    nwcol_d = din("nw_col", [128, 16])
    win_d = din("w_in", [2, D, DIN])
    wout_d = din("w_out", [2, D, D])
    fnw_d = din("fnw", [1, D])
    rows_d = din("rows", [1, 2 * NROW])
    dlam_d = din("dlam", [1, 256])
    cwcol_d = din("cw_col", [128, 60])
    cbcol_d = din("cb_col", [128, 12])
    multA_d = din("multA", [128, TW])
    rpbm_d = din("rpbm", [2, 4, 128, 9 * 768])
    out_d = nc.dram_tensor("out", [T, D], F32, kind="ExternalOutput").ap()
    x1_d = nc.dram_tensor("x1s", [T, D], F32, kind="Internal").ap()
    if dbg:
        dbg_ycat = nc.dram_tensor("dbg_ycat", [2, 4, T, 256], F32, kind="ExternalOutput").ap()
        dbg_h = nc.dram_tensor("dbg_h", [128, 8 * T], F32, kind="ExternalOutput").ap()

    slopes_all = [2.0 ** (-8.0 * i / 8) for i in range(1, 9)]
    slopes_a = slopes_all[0::2]
    slopes_b = slopes_all[1::2]

    with ExitStack() as es:
        kb = KB(nc, es)
        op, dma = kb.op, kb.dma

        def chk(name):
            if stop == name:
                kb.stopped = True
        try:
            _body(nc, es, kb, dbg, chk, locals())
        except _Stop:
            pass
        sp = kb.eng["sp"]
        for d in kb.dsems:
            if d.last is not None:
                kb._wait(sp, d.last)
    return nc


def _body(nc, es, kb, dbg, chk, env):
    globals_ = env
    (x_d, ccol_d, cst_d, adaw_d, adab_d, nwcol_d, win_d, wout_d, fnw_d, rows_d, dlam_d, cwcol_d, cbcol_d, multA_d, rpbm_d,
     out_d, x1_d, slopes_a, slopes_b) = [env[k] for k in (
        "x_d", "ccol_d", "cst_d", "adaw_d", "adab_d", "nwcol_d", "win_d", "wout_d", "fnw_d", "rows_d", "dlam_d", "cwcol_d",
        "cbcol_d", "multA_d", "rpbm_d", "out_d", "x1_d", "slopes_a", "slopes_b")]
    dbg_ycat = env.get("dbg_ycat")
    dbg_h = env.get("dbg_h")
    op, dma = kb.op, kb.dma

    class scope(ExitStack):
        def __exit__(self, *a):
            kb.barrier()
            return ExitStack.__exit__(self, *a)
    if True:

        uid = [0]

        def sb(name, shape, dt, stack=None):
            uid[0] += 1
            return (stack or es).enter_context(nc.sbuf_tensor("s%d_%s" % (uid[0], name), list(shape), dt))

        PS = [es.enter_context(nc.psum_tensor("ps%d" % i, [128, 512], F32)) for i in range(8)]

        def pk(i):
            return ("ps", i)

        hT = sb("hT", [128, 8, T], BF16)
        ycatT = sb("ycatT", [128, 8, T], BF16)
        cst = sb("cst", [128, 770], F32)
        identb = sb("identb", [128, 128], BF16)
        rowsb = sb("rowsb", [128, 2 * NROW], F32)
        nwcol = sb("nwcol", [128, 16], F32)
        Acol = sb("Acol", [128, 16], F32)
        Bcol = sb("Bcol", [128, 16], F32)
        gatebc = sb("gatebc", [128, 2, D], F32)
        sccol = sb("sccol", [128, 8], BF16)
        Wt = sb("Wt", [128, 8, 1032], BF16)
        nlam = sb("nlam", [128, 2], F32)
        small = sb("small", [128, 64], F32)
        identf = cst[:, 0:128]
        U_le = cst[:, 128:256]
        U_gt = cst[:, 256:384]
        U_lt = cst[:, 384:512]
        U_ge = cst[:, 512:640]
        onesf = cst[:, 640:768]
        kmask = [cst[:, 768:769], cst[:, 769:770]]

        dma("sp", cst[:], cst_d, w=["cst"])
        dma("sp", nwcol[:], nwcol_d, w=["nwcol"])
        dma("sp", rowsb[:], rows_d.partition_broadcast(128), w=["rowsb"])
        op("dve", lambda: nc.vector.tensor_copy(out=identb[:], in_=identf), r=["cst"], w=["identb"])

        def load_w(wt, key, l, segs):
            for (dc, sc, n) in segs:
                src = win_d[l, :, sc:sc + n].rearrange("(c p) n -> p c n", p=128)
                dma("pool", wt[:, :, dc:dc + n], src, w=[key])

        MIX_ORDER = [(l_, m_) for l_ in range(2) for m_ in ("A", "B", "C", "D")]

        def issue_weights(l_, m_):
            if m_ == "C":
                load_w(Wt, "W", l_, [(0, 2048, 256), (256, 3072, 8), (264, 2304, 768)])
            else:
                base_ = {"A": 0, "B": 1024, "D": 3080}[m_]
                load_w(Wt, "W", l_, [(0, base_, 512), (512, base_ + 512, 512)])

        def issue_next(l_, m_):
            i_ = MIX_ORDER.index((l_, m_))
            if i_ + 1 < len(MIX_ORDER):
                issue_weights(*MIX_ORDER[i_ + 1])

        def rsqrt_col(dst, src, scale, tagr, tagw):
            op("dve", lambda: nc.vector.tensor_scalar(out=dst, in0=src, scalar1=scale, scalar2=EPS, op0=ALU.mult,
                                                      op1=ALU.add), r=tagr, w=tagw)
            op("act", lambda: nc.scalar.activation(out=dst, in_=dst, func=AF.Ln), r=tagw, w=tagw)
            op("act", lambda: nc.scalar.activation(out=dst, in_=dst, func=AF.Exp, scale=-0.5), r=tagw, w=tagw)

        def norm_pre(xin, xkey, ssc, sskey, xn_ap, xnkey, junk_ap, jkey):
            op("act", lambda: nc.scalar.activation(out=junk_ap, in_=xin, func=AF.Square, accum_out=ssc),
               r=[xkey], w=[jkey, sskey])
            rsqrt_col(ssc, ssc, 1.0 / D, [sskey], [sskey])
            op("dve", lambda: nc.vector.tensor_scalar(out=xn_ap, in0=xin, scalar1=ssc, scalar2=None, op0=ALU.mult),
               r=[xkey, sskey], w=[xnkey])

        def norm_post(l, xn_ap, xnkey, blk, bank):
            psb = PS[bank][:].bitcast(BF16)
            for c in range(8):
                op("pe", lambda c=c: nc.tensor.transpose(psb[:, c * 128:(c + 1) * 128], xn_ap[:, c * 128:(c + 1) * 128],
                                                         identb[:]), r=[xnkey, "identb"], w=[pk(bank)], sig=(c == 7))
            for c in range(8):
                if (c + blk) % 2 == 0:
                    op("dve", lambda c=c: nc.vector.tensor_scalar(
                        out=hT[:, c, blk * 128:(blk + 1) * 128], in0=psb[:, c * 128:(c + 1) * 128],
                        scalar1=Acol[:, l * 8 + c:l * 8 + c + 1], scalar2=Bcol[:, l * 8 + c:l * 8 + c + 1],
                        op0=ALU.mult, op1=ALU.add), r=[pk(bank), "Acol", "Bcol"], w=[("hT", blk, 0)])
                else:
                    op("act", lambda c=c: nc.scalar.activation(
                        out=hT[:, c, blk * 128:(blk + 1) * 128], in_=psb[:, c * 128:(c + 1) * 128], func=AF.Identity,
                        scale=Acol[:, l * 8 + c:l * 8 + c + 1], bias=Bcol[:, l * 8 + c:l * 8 + c + 1]),
                       r=[pk(bank), "Acol", "Bcol"], w=[("hT", blk, 1)])

        def norm_post4(l, xn_list, xnkeys, blk0, banks):
            for c in range(8):
                bank = banks[c % len(banks)]
                psb = PS[bank][:].bitcast(BF16)
                for t in range(4):
                    op("pe", lambda t=t, c=c, psb=psb: nc.tensor.transpose(
                        psb[:, t * 128:(t + 1) * 128], xn_list[t][:, c * 128:(c + 1) * 128], identb[:]),
                       r=[xnkeys[t], "identb"], w=[pk(bank)], sig=(t == 3))
                dst = hT[:, c, blk0 * 128:(blk0 + 4) * 128]
                wk = [("hT", blk0 + t, c % 2) for t in range(4)]
                if c % 2 == 0:
                    op("dve", lambda c=c, psb=psb, dst=dst: nc.vector.tensor_scalar(
                        out=dst, in0=psb[:, 0:512], scalar1=Acol[:, l * 8 + c:l * 8 + c + 1],
                        scalar2=Bcol[:, l * 8 + c:l * 8 + c + 1], op0=ALU.mult, op1=ALU.add),
                       r=[pk(bank), "Acol", "Bcol"], w=wk)
                else:
                    op("act", lambda c=c, psb=psb, dst=dst: nc.scalar.activation(
                        out=dst, in_=psb[:, 0:512], func=AF.Identity, scale=Acol[:, l * 8 + c:l * 8 + c + 1],
                        bias=Bcol[:, l * 8 + c:l * 8 + c + 1]), r=[pk(bank), "Acol", "Bcol"], w=wk)

        hT_all = [("hT", b, p_) for b in range(NB) for p_ in range(2)]

        def mod_gen(l, stk, pbanks=(4, 5), cbanks=(6, 7), dve_cast=False):
            NBF = 4
            adab3 = [sb("adab3_%d" % i, [1, D], F32, stk) for i in range(2)]
            modrow3 = sb("modrow3", [1, D], F32, stk)
            wp = [sb("wp%d" % i, [128, 8, 128], F32, stk) for i in range(NBF)]
            wpb = [sb("wpb%d" % i, [128, 8, 128], BF16, stk) for i in range(NBF)]

            def stage_a(p):
                third, pc = divmod(p, 8)
                k = p % NBF
                if pc == 0:
                    dma("sp", adab3[third % 2][:], adab_d[:, l * 3 * D + third * D: l * 3 * D + (third + 1) * D],
                        w=[("adab3", third % 2)])
                dma("sp", wp[k][:].rearrange("p c n -> p (c n)"), adaw_d[l, p], w=[("wp", k)])
                if dve_cast and p % 2 == 1:
                    op("dve", lambda: nc.vector.tensor_copy(out=wpb[k][:], in_=wp[k][:]), r=[("wp", k)],
                       w=[("wpb", k)])
                else:
                    op("pool", lambda: nc.gpsimd.tensor_copy(out=wpb[k][:], in_=wp[k][:]), r=[("wp", k)],
                       w=[("wpb", k)])
            for p0 in range(3):
                stage_a(p0)
            for third in range(3):
                for pc in range(8):
                    p = third * 8 + pc
                    if p + 3 < 24:
                        stage_a(p + 3)
                    k = p % NBF
                    bank = pbanks[p % 2]
                    for c in range(8):
                        op("pe", lambda c=c, k=k, bank=bank: nc.tensor.matmul(
                            PS[bank][0:1, 0:128], sccol[:, c:c + 1], wpb[k][:, c, :], start=(c == 0), stop=(c == 7),
                            skip_group_check=True),
                           r=["sccol", ("wpb", k)], w=[pk(bank)], sig=(c == 7))
                    op("dve", lambda pc=pc, bank=bank, third=third: nc.vector.tensor_tensor(
                        out=modrow3[0:1, pc * 128:(pc + 1) * 128], in0=PS[bank][0:1, 0:128],
                        in1=adab3[third % 2][0:1, pc * 128:(pc + 1) * 128], op=ALU.add),
                       r=[pk(bank), ("adab3", third % 2)], w=["modrow3"])
                    if pc < 7:
                        yield (third, pc)
                if third < 2:
                    for k8 in range(8):
                        op("pe", lambda k8=k8: nc.tensor.matmul(
                            PS[cbanks[0]][:, k8:k8 + 1], modrow3[0:1, k8 * 128:(k8 + 1) * 128], onesf[0:1, 0:1],
                            start=(k8 == 0), stop=(k8 == 7), skip_group_check=True),
                           r=["modrow3", "cst"], w=[pk(cbanks[0])], sig=(k8 == 7))
                    if third == 0:
                        op("dve", lambda: nc.vector.tensor_copy(out=Bcol[:, l * 8:(l + 1) * 8], in_=PS[cbanks[0]][:, 0:8]),
                           r=[pk(cbanks[0])], w=["Bcol"])
                    else:
                        op("dve", lambda: nc.vector.scalar_tensor_tensor(
                            out=Acol[:, l * 8:(l + 1) * 8], in0=PS[cbanks[0]][:, 0:8], scalar=1.0,
                            in1=nwcol[:, l * 8:(l + 1) * 8], op0=ALU.add, op1=ALU.mult),
                           r=[pk(cbanks[0]), "nwcol"], w=["Acol"])
                else:
                    for n in range(2):
                        op("pe", lambda n=n: nc.tensor.matmul(
                            PS[cbanks[1]][:, :], onesf[0:1, :], modrow3[0:1, n * 512:(n + 1) * 512],
                            start=True, stop=True, skip_group_check=True), r=["modrow3", "cst"], w=[pk(cbanks[1])])
                        op("dve", lambda n=n: nc.vector.tensor_copy(out=gatebc[:, l, n * 512:(n + 1) * 512],
                                                                     in_=PS[cbanks[1]][:, :]), r=[pk(cbanks[1])], w=["gatebc"])
                yield (third, 7)

        with scope() as s0:
            ccol = sb("ccol", [128, 8], F32, s0)
            dlam = sb("dlam", [1, 256], F32, s0)
            ltmp = sb("ltmp", [1, 136], F32, s0)
            xn_all = sb("xn_all", [128, NB, D], BF16, s0)
            junk0 = [sb("junk0_%d" % i, [128, D], BF16, s0) for i in range(2)]
            ssq0 = sb("ssq0", [128, NB], F32, s0)
            xin = [sb("xin%d" % i, [128, D], F32, s0) for i in range(3)]
            pre_blocks = list(range(NB))

            def xload0(blk):
                dma("act", xin[blk % 3][:], x_d[blk * 128:(blk + 1) * 128, :], w=[("xin", blk % 3)])
            xload0(0)
            xload0(1)

            def pre_norm_some(n):
                for _ in range(n):
                    if not pre_blocks:
                        return
                    blk = pre_blocks.pop(0)
                    k3 = blk % 3
                    if blk + 2 < NB:
                        xload0(blk + 2)
                    norm_pre(xin[k3][:], ("xin", k3), ssq0[:, blk:blk + 1], ("ssq0", blk), xn_all[:, blk, :],
                             ("xn_all", blk), junk0[blk % 2][:], ("junk0", blk % 2))
            dma("sp", ccol[:], ccol_d, w=["ccol"])
            dma("sp", dlam[:], dlam_d, w=["dlam"])
            op("act", lambda: nc.scalar.activation(out=sccol[:], in_=ccol[:], func=AF.Silu), r=["ccol"], w=["sccol"])
            for (third, pc) in mod_gen(0, s0, dve_cast=True):
                if third == 1 and pc == 0:
                    issue_weights(0, "A")
                if third < 2:
                    pre_norm_some(1)
                elif pc % 2 == 0:
                    g_ = pc // 2
                    pre_norm_some(NB)
                    norm_post4(0, [xn_all[:, g_ * 4 + t, :] for t in range(4)],
                               [("xn_all", g_ * 4 + t) for t in range(4)], g_ * 4, [0, 1, 2, 3])
            for l in range(2):
                lam_init = 0.8 - 0.6 * math.exp(-0.3 * l)
                dl = dlam[0:1, l * 128:(l + 1) * 128].rearrange("p (a t b) -> p a t b", a=2, t=2)
                op("dve", lambda dl=dl: nc.vector.tensor_tensor(
                    out=ltmp[0:1, 0:64].rearrange("p (a b) -> p a b", a=2), in0=dl[:, :, 0, :], in1=dl[:, :, 1, :],
                    op=ALU.mult), r=["dlam"], w=["ltmp"])
                op("dve", lambda: nc.vector.tensor_reduce(
                    out=ltmp[0:1, 64:66], in_=ltmp[0:1, 0:64].rearrange("p (a b) -> p a b", a=2), axis=AX.X,
                    op=ALU.add), r=["ltmp"], w=["ltmp"])
                op("act", lambda: nc.scalar.activation(out=ltmp[0:1, 66:68], in_=ltmp[0:1, 64:66], func=AF.Exp),
                   r=["ltmp"], w=["ltmp"])
                op("dve", lambda lam_init=lam_init: nc.vector.scalar_tensor_tensor(
                    out=ltmp[0:1, 68:69], in0=ltmp[0:1, 67:68], scalar=-lam_init, in1=ltmp[0:1, 66:67],
                    op0=ALU.add, op1=ALU.subtract), r=["ltmp"], w=["ltmp"])
                op("pe", lambda: nc.tensor.matmul(PS[7][:, 0:1], onesf[0:1, :], ltmp[0:1, 68:69], start=True,
                                                  stop=True, skip_group_check=True), r=["ltmp", "cst"], w=[pk(7)])
                op("dve", lambda l=l: nc.vector.tensor_copy(out=nlam[:, l:l + 1], in_=PS[7][:, 0:1]),
                   r=[pk(7)], w=["nlam"])

        chk('startup')

        proj_rot = [0]
        proj_banks = [[0, 1, 2, 3, 4, 5, 6]]

        def proj_fm(wt, wkey, col0, evac):
            for tq in range(4):
                bank = proj_banks[0][proj_rot[0] % len(proj_banks[0])]
                proj_rot[0] += 1
                for c in range(8):
                    op("pe", lambda c=c, bank=bank: nc.tensor.matmul(
                        PS[bank][:, :], wt[:, c, col0:col0 + 128], hT[:, c, tq * 512:(tq + 1) * 512],
                        start=(c == 0), stop=(c == 7), skip_group_check=True),
                       r=[wkey] + hT_all[tq * 8:(tq + 1) * 8], w=[pk(bank)], sig=(c == 7))
                evac(PS[bank], tq, bank)

        def proj_tm(wt, wkey, col0, n, evac):
            for blk in range(NB):
                bank = proj_banks[0][proj_rot[0] % len(proj_banks[0])]
                proj_rot[0] += 1
                for c in range(8):
                    op("pe", lambda c=c, bank=bank: nc.tensor.matmul(
                        PS[bank][:, 0:n], hT[:, c, blk * 128:(blk + 1) * 128], wt[:, c, col0:col0 + n],
                        start=(c == 0), stop=(c == 7), skip_group_check=True),
                       r=[wkey, ("hT", blk, 0), ("hT", blk, 1)], w=[pk(bank)], sig=(c == 7))
                evac(PS[bank], blk, bank)

        alt = [0]

        def copy_alt(out, in_, r, w):
            alt[0] += 1
            if alt[0] % 2:
                op("dve", lambda: nc.vector.tensor_copy(out=out, in_=in_), r=r, w=w)
            else:
                op("act", lambda: nc.scalar.copy(out=out, in_=in_), r=r, w=w)

        pending = []

        def run_pending(n=1):
            for _ in range(n):
                if pending:
                    pending.pop(0)()

        orot = [0]
        srot = [0]

        def attention_task(KT, QT, V, steps, scale, mults, rkeys, epilogue, Pt):
            qbase = (steps[0][1] // 1024) * 1024
            ob = [0, 1]
            orot[0] += 1
            started = set()
            nsteps = len(steps)
            last_in_bank = {}
            for si, (j, qa, qb) in enumerate(steps):
                last_in_bank[(qa - qbase) // 512] = si
            sinfo = {}

            def emit_S(si):
                j, qa, qb = steps[si]
                n = qb - qa
                sbank = 2 + srot[0] % 5
                pidx = srot[0] % 5
                srot[0] += 1
                op("pe", lambda: nc.tensor.matmul(PS[sbank][:, 0:n], KT(j), QT(qa, qb), start=True, stop=True,
                                                  skip_group_check=True), r=rkeys, w=[pk(sbank)])
                op("act", lambda: nc.scalar.activation(out=Pt[pidx][:, 0:n], in_=PS[sbank][:, 0:n], func=AF.Exp,
                                                       scale=scale), r=[pk(sbank)], w=[("P", pidx)])
                for (en, fn, keys) in mults:
                    ap = fn(j, qa, qb)
                    if en == "dve":
                        op("dve", lambda ap=ap: nc.vector.tensor_tensor(out=Pt[pidx][:, 0:n], in0=Pt[pidx][:, 0:n],
                                                                        in1=ap, op=ALU.mult),
                           r=[("P", pidx)] + keys, w=[("P", pidx)])
                    else:
                        op("pool", lambda ap=ap: nc.gpsimd.tensor_tensor(out=Pt[pidx][:, 0:n], in0=Pt[pidx][:, 0:n],
                                                                         in1=ap, op=ALU.mult),
                           r=[("P", pidx)] + keys, w=[("P", pidx)])
                sinfo[si] = pidx

            def emit_PV(si):
                j, qa, qb = steps[si]
                n = qb - qa
                b = (qa - qbase) // 512
                bank = ob[b]
                o0 = qa - qbase - b * 512
                first = b not in started
                started.add(b)
                pidx = sinfo[si]
                op("pe", lambda: nc.tensor.matmul(PS[bank][:, o0:o0 + n], V(j), Pt[pidx][:, 0:n], start=first,
                                                  stop=(last_in_bank[b] == si), skip_group_check=True),
                   r=[("P", pidx)] + rkeys, w=[pk(bank)], sig=(last_in_bank[b] == si))

            LA = 4
            for si in range(min(LA, nsteps)):
                emit_S(si)
            for si in range(nsteps):
                if si + LA < nsteps:
                    emit_S(si + LA)
                emit_PV(si)
                run_pending(1)
            epilogue(ob, qbase)

        def make_epilogue(OTs, handlers):
            erot = [0]

            def epilogue(ob, qbase, h, i):
                while pending:
                    run_pending(1)
                k = erot[0] % 2
                erot[0] += 1
                for b in range(2):
                    op("dve", lambda b=b: nc.vector.tensor_copy(out=OTs[k][0:65, b * 512:(b + 1) * 512],
                                                                in_=PS[ob[b]][0:65, :]),
                       r=[pk(ob[b])], w=[("OTs", k)])

                def tstage(g):
                    def f():
                        for t in range(4):
                            blk = g * 4 + t
                            op("pe", lambda t=t, blk=blk: nc.tensor.transpose(
                                PS[7][:, t * 65:(t + 1) * 65], OTs[k][0:65, blk * 128:(blk + 1) * 128],
                                identf[0:65, 0:65]), r=[("OTs", k), "cst"], w=[pk(7)], sig=(t == 3))
                    return f
                psT = PS[7][:, 0:260].rearrange("p (g d) -> p g d", d=65)
                hs = [handlers(psT, qbase // 128 + g * 4, g, h, i) for g in range(2)]
                if len(hs[0]) == 1:
                    seq = [tstage(0), hs[0][0], tstage(1), hs[1][0]]
                else:
                    seq = [tstage(0), hs[0][0], tstage(1), hs[0][1], hs[1][0], hs[0][2], hs[1][1], hs[1][2]]
                pending.extend(seq)
            return epilogue

        def back_transpose(Ytm, ykey, chunk0, nchunks=2):
            for cc in range(nchunks):
                for g in range(4):
                    bank = 4 + g
                    psb = PS[bank][:].bitcast(BF16)
                    for t in range(4):
                        blk = g * 4 + t
                        op("pe", lambda t=t, blk=blk, psb=psb: nc.tensor.transpose(
                            psb[:, t * 128:(t + 1) * 128], Ytm[:, blk, cc * 128:(cc + 1) * 128], identb[:]),
                           r=[ykey, "identb"], w=[pk(bank)], sig=(t == 3))
                    copy_alt(ycatT[:, chunk0 + cc, g * 512:(g + 1) * 512], psb[:, 0:512], [pk(bank)],
                             [("ycatT", chunk0 + cc, g)])

        def dbg_dump(l, m, Ytm, ykey, stk):
            if not dbg:
                return
            tmp = sb("dbgtmp%d%d" % (l, m), [128, 4, 256], F32, stk)
            for q in range(4):
                op("dve", lambda q=q: nc.vector.tensor_copy(out=tmp[:], in_=Ytm[:, q * 4:(q + 1) * 4, :]), r=[ykey],
                   w=["dbgtmp"])
                dma("sp", dbg_ycat[l, m, q * 512:(q + 1) * 512, :].rearrange("(b p) n -> p b n", p=128), tmp[:],
                    r=["dbgtmp"], w=["dbgout"])

        def ssd(l):
            R0 = l * NROW
            alog_bc = rowsb[:, R0:R0 + 8]
            dtb_bc = rowsb[:, R0 + 8:R0 + 16]
            dsk_bc = rowsb[:, R0 + 16:R0 + 20]
            snw_bc = rowsb[:, R0 + 20:R0 + 276]

            def ph(bank, half):
                return ("ps", bank)

            with scope() as sc:
                Gc = sb("Gc", [128, NB, 256], BF16, sc)
                dtt = sb("dtt", [128, NB, 8], F32, sc)
                att = sb("att", [128, 2, NB, 4], F32, sc)
                W4 = sb("W4", [128, NB, 16], F32, sc)
                smallc = sb("smallc", [128, 384], F32, sc)
                nA = sb("nA", [128, 8], F32, sc)
                Xtm = sb("Xtm", [128, NB, 256], BF16, sc)
                Btm = sb("Btm", [128, NB, 256], BF16, sc)
                BCT = sb("BCT", [128, 4, T], BF16, sc)
                with scope() as spj:
                    wt = Wt
                    pre = [sb("pre%d" % i, [128, 2052], BF16, spj) for i in range(2)]
                    xsT = [sb("xsT%d" % i, [128, T], BF16, spj) for i in range(2)]
                    diagw = sb("diagw", [128, 30, 128], BF16, spj)
                    cwcol = sb("cwcol", [128, 60], F32, spj)
                    cbcol = sb("cbcol", [128, 12], F32, spj)
                    dma("sp", cwcol[:], cwcol_d, w=["cwcol"])
                    dma("sp", cbcol[:], cbcol_d, w=["cbcol"])
                    for k in range(2):
                        op("pool", lambda k=k: nc.gpsimd.memset(pre[k][:], 0.0), w=[("pre", k)])
                    for jc in range(30):
                        op("dve", lambda jc=jc: nc.vector.tensor_scalar(
                            out=diagw[:, jc, :], in0=identb[:], scalar1=cwcol[:, l * 30 + jc:l * 30 + jc + 1],
                            scalar2=None, op0=ALU.mult), r=["identb", "cwcol"], w=["diagw"])

                    def evac_zdt(ps, blk, bank):
                        op("act", lambda: nc.scalar.activation(out=Gc[:, blk, :], in_=ps[:, 0:256], func=AF.Silu),
                           r=[pk(bank)], w=["Gc"])
                        op("dve", lambda: nc.vector.tensor_tensor(out=dtt[:, blk, :], in0=ps[:, 256:264], in1=dtb_bc,
                                                                  op=ALU.add), r=[pk(bank), "rowsb"], w=["dtt"])
                    proj_tm(wt, "W", 0, 264, evac_zdt)
                    op("act", lambda: nc.scalar.activation(out=dtt[:], in_=dtt[:], func=AF.Exp), r=["dtt"], w=["dtt"])
                    op("dve", lambda: nc.vector.tensor_scalar(out=dtt[:], in0=dtt[:], scalar1=1.0, scalar2=None,
                                                              op0=ALU.add), r=["dtt"], w=["dtt"])
                    op("act", lambda: nc.scalar.activation(out=dtt[:], in_=dtt[:], func=AF.Ln), r=["dtt"], w=["dtt"])
                    op("act", lambda: nc.scalar.activation(out=nA[:], in_=alog_bc, func=AF.Exp), r=["rowsb"], w=["nA"])
                    for d_ in range(2):
                        op("dve", lambda d_=d_: nc.vector.scalar_tensor_tensor(
                            out=att[:, d_, :, :], in0=dtt[:, :, d_ * 4:(d_ + 1) * 4], scalar=-1.0,
                            in1=nA[:, d_ * 4:(d_ + 1) * 4].unsqueeze(1).broadcast_to([128, NB, 4]),
                            op0=ALU.mult, op1=ALU.mult), r=["dtt", "nA"], w=["att"])

                    mg = mod_gen(1, spj, pbanks=(0, 1), cbanks=(2, 2)) if l == 0 else iter(())
                    proj_banks[0] = [3, 4, 5, 6] if l == 0 else [0, 1, 2, 3, 4, 5, 6]
                    def proj_part(ch):
                        k = ch % 2
                        proj_fm(wt, "W", 264 + ch * 128,
                                lambda ps, tq, bank, k=k: copy_alt(pre[k][:, 2 + tq * 512: 2 + (tq + 1) * 512], ps[:, :],
                                                                   [pk(bank)], [("pre", k)]))

                    def conv_part(ch):
                        k = ch % 2
                        for tq in range(4):
                            next(mg, None)
                            bank = proj_banks[0][proj_rot[0] % len(proj_banks[0])]
                            proj_rot[0] += 1
                            for j in range(5):
                                op("pe", lambda j=j, bank=bank: nc.tensor.matmul(
                                    PS[bank][:, :], diagw[:, j * 6 + ch, :], pre[k][:, tq * 512 + j: tq * 512 + j + 512],
                                    start=(j == 0), stop=(j == 4), skip_group_check=True),
                                   r=["diagw", ("pre", k)], w=[pk(bank)], sig=(j == 4))
                            if ch < 2:
                                dst, dkey = xsT[ch][:, tq * 512:(tq + 1) * 512], ("xsT", ch)
                            else:
                                dst, dkey = BCT[:, ch - 2, tq * 512:(tq + 1) * 512], ("BCT", ch - 2)
                            op("act", lambda bank=bank, dst=dst: nc.scalar.activation(
                                out=dst, in_=PS[bank][:, :], func=AF.Silu, bias=cbcol[:, l * 6 + ch:l * 6 + ch + 1]),
                               r=[pk(bank), "cbcol"], w=[dkey])
                        if ch < 4:
                            srcT = (lambda t0, ch=ch: xsT[ch][:, t0:t0 + 128]) if ch < 2 else \
                                (lambda t0, ch=ch: BCT[:, ch - 2, t0:t0 + 128])
                            skey = ("xsT", ch) if ch < 2 else ("BCT", ch - 2)
                            dstT = Xtm if ch < 2 else Btm
                            cc = ch % 2
                            psb = PS[7][:].bitcast(BF16)
                            for g in range(4):
                                for t in range(4):
                                    blk = g * 4 + t
                                    op("pe", lambda t=t, blk=blk: nc.tensor.transpose(
                                        psb[:, t * 128:(t + 1) * 128], srcT(blk * 128), identb[:]),
                                       r=[skey, "identb"], w=[pk(7)], sig=(t == 3))
                                copy_alt(dstT[:, g * 4:(g + 1) * 4, cc * 128:(cc + 1) * 128],
                                         psb[:, 0:512].rearrange("p (t n) -> p t n", t=4), [pk(7)],
                                         ["Xtm" if ch < 2 else "Btm"])

                    proj_part(0)
                    for ch in range(6):
                        if ch + 1 < 6:
                            proj_part(ch + 1)
                        conv_part(ch)

                    for _ in mg:
                        pass
                proj_banks[0] = [0, 1, 2, 3, 4, 5, 6]
                issue_next(l, "C")
                Y = sb("Yc", [128, NB, 256], F32, sc)
                Ytm = sb("YtmC", [128, NB, 256], BF16, sc)
                attf = [att[:, d_, :, :].rearrange("p c h -> p (c h)") for d_ in range(2)]
                for ki, (U, d_) in enumerate(((U_gt, 0), (U_lt, 1), (U_le, 0), (U_ge, 1))):
                    op("pe", lambda U=U, d_=d_, ki=ki: nc.tensor.matmul(
                        PS[6][:, ki * 64:(ki + 1) * 64], U, attf[d_], start=(ki == 0), stop=True,
                        skip_group_check=True), r=["att", "cst"], w=[pk(6)])
                op("pe", lambda: nc.tensor.matmul(
                    PS[6][:, 256:384], onesf, att[:].rearrange("p d c h -> p (d c h)"), start=False, stop=True,
                    skip_group_check=True), r=["att", "cst"], w=[pk(6)])
                op("act", lambda: nc.scalar.activation(out=smallc[:], in_=PS[6][:, 0:384], func=AF.Exp),
                   r=[pk(6)], w=["smallc"])
                op("dve", lambda: nc.vector.tensor_copy(out=W4[:, :, 0:8], in_=dtt[:]), r=["dtt"], w=["W4"])
                for d_ in range(2):
                    op("dve", lambda d_=d_: nc.vector.tensor_tensor(
                        out=W4[:, :, 8 + d_ * 4:12 + d_ * 4], in0=dtt[:, :, d_ * 4:(d_ + 1) * 4],
                        in1=smallc[:, d_ * 64:(d_ + 1) * 64].rearrange("p (c h) -> p c h", h=4), op=ALU.mult),
                       r=["dtt", "smallc"], w=["W4"])

                with scope() as ssw:
                    aU = [sb("aU%d" % i, [128, 4, 128], F32, ssw) for i in range(4)]
                    eD = [sb("eD%d" % i, [128, 4, 128], F32, ssw) for i in range(4)]
                    CBm = [sb("CBm%d" % i, [128, 2, 128], F32, ssw) for i in range(4)]
                    MT = [sb("MT%d" % i, [128, 4, 128], BF16, ssw) for i in range(4)]
                    Xw = [sb("Xw%d" % i, [128, 2, 256], BF16, ssw) for i in range(4)]
                    tz = [sb("tz%d" % i, [128, 4, 64], F32, ssw) for i in range(2)]
                    Sd = [sb("Sst%d" % i, [128, 256], F32, ssw) for i in range(2)]
                    Sbfd = [sb("Sbf%d" % i, [128, 256], BF16, ssw) for i in range(2)]
                    for dr_ in range(2):
                        op("dve", lambda dr_=dr_: nc.vector.memset(Sd[dr_][:], 0.0), w=[("S", dr_)])
                        op("dve", lambda dr_=dr_: nc.vector.memset(Sbfd[dr_][:], 0.0), w=[("Sbf", dr_)])

                    def prep(ci, dr_):
                        U1 = U_gt if dr_ == 0 else U_lt
                        U2 = U_le if dr_ == 0 else U_ge
                        d4 = dr_ * 4
                        c = ci if dr_ == 0 else NB - 1 - ci
                        k = dr_ * 2 + ci % 2
                        bD, bC = dr_, 2 + dr_
                        tsl = slice(c * 128, (c + 1) * 128)
                        op("pool", lambda: nc.gpsimd.tensor_tensor(
                            out=aU[k][:], in0=U2.unsqueeze(1).broadcast_to([128, 4, 128]),
                            in1=att[:, dr_, c, :].unsqueeze(2).broadcast_to([128, 4, 128]), op=ALU.mult),
                           r=["att", "cst"], w=[("aU", k)])
                        op("pe", lambda: nc.tensor.matmul(PS[bD][:, :], U1, aU[k][:].rearrange("p h l -> p (h l)"),
                                                          start=True, stop=True, skip_group_check=True),
                           r=[("aU", k), "cst"], w=[pk(bD)])
                        op("act", lambda: nc.scalar.activation(out=eD[k][:].rearrange("p h l -> p (h l)"),
                                                               in_=PS[bD][:, :], func=AF.Exp),
                           r=[pk(bD)], w=[("eD", k)])
                        for g in range(2):
                            op("pe", lambda g=g: nc.tensor.matmul(PS[bC][:, g * 128:(g + 1) * 128], BCT[:, g, tsl],
                                                                  BCT[:, 2 + g, tsl], start=True, stop=True,
                                                                  skip_group_check=True),
                               r=[("BCT", g), ("BCT", 2 + g)], w=[pk(bC)], sig=(g == 1))
                        op("dve", lambda: nc.vector.tensor_tensor(
                            out=CBm[k][:], in0=PS[bC][:, 0:256].rearrange("p (g l) -> p g l", g=2),
                            in1=U2.unsqueeze(1).broadcast_to([128, 2, 128]), op=ALU.mult),
                           r=[pk(bC), "cst"], w=[("CBm", k)])
                        op("dve", lambda: nc.vector.tensor_tensor(
                            out=MT[k][:].rearrange("p (g e) l -> p g e l", g=2),
                            in0=eD[k][:].rearrange("p (g e) l -> p g e l", g=2),
                            in1=CBm[k][:].unsqueeze(2).broadcast_to([128, 2, 2, 128]), op=ALU.mult),
                           r=[("eD", k), ("CBm", k)], w=[("MT", k)])
                        op("pool", lambda: nc.gpsimd.tensor_tensor(
                            out=Xw[k][:].rearrange("p k (h d) -> p k h d", h=4),
                            in0=Xtm[:, c, :].rearrange("p (h d) -> p h d", h=4).unsqueeze(1).broadcast_to(
                                [128, 2, 4, 64]),
                            in1=W4[:, c, :].rearrange("p (k e h) -> p k e h", k=2, e=2)[:, :, dr_, :].unsqueeze(
                                3).broadcast_to([128, 2, 4, 64]), op=ALU.mult),
                           r=["Xtm", "W4"], w=[("Xw", k)])

                    def main(ci, dr_):
                        d4 = dr_ * 4
                        S, Sbf = Sd[dr_], Sbfd[dr_]
                        Sk, Sbk = ("S", dr_), ("Sbf", dr_)
                        c = ci if dr_ == 0 else NB - 1 - ci
                        k = dr_ * 2 + ci % 2
                        kt = dr_
                        bY, bS = 4 + dr_, 6 + dr_
                        tsl = slice(c * 128, (c + 1) * 128)
                        for h in range(4):
                            op("pe", lambda h=h: nc.tensor.matmul(PS[bY][:, h * 64:(h + 1) * 64], MT[k][:, h, :],
                                                                  Xw[k][:, 0, h * 64:(h + 1) * 64], start=True,
                                                                  stop=True, skip_group_check=True),
                               r=[("MT", k), ("Xw", k)], w=[pk(bY)], sig=False)
                        for h in range(4):
                            op("pe", lambda h=h: nc.tensor.matmul(PS[bY][:, 256 + h * 64:256 + (h + 1) * 64],
                                                                  BCT[:, 2 + h // 2, tsl], Sbf[:, h * 64:(h + 1) * 64],
                                                                  start=True, stop=True, skip_group_check=True),
                               r=[("BCT", 2), ("BCT", 3), Sbk], w=[pk(bY)], sig=(h == 3))
                        for g in range(2):
                            op("pe", lambda g=g: nc.tensor.matmul(PS[bS][:, g * 128:(g + 1) * 128],
                                                                  Btm[:, c, g * 128:(g + 1) * 128],
                                                                  Xw[k][:, 1, g * 128:(g + 1) * 128], start=True,
                                                                  stop=True, skip_group_check=True),
                               r=["Btm", ("Xw", k)], w=[pk(bS)], sig=(g == 1))
                        op("pool", lambda: nc.gpsimd.tensor_tensor(
                            out=S[:].rearrange("p (h d) -> p h d", h=4), in0=S[:].rearrange("p (h d) -> p h d", h=4),
                            in1=smallc[:, 256 + dr_ * 64 + c * 4:256 + dr_ * 64 + c * 4 + 4].unsqueeze(2).broadcast_to([128, 4, 64]), op=ALU.mult),
                           r=[Sk, "smallc"], w=[Sk])
                        op("dve", lambda: nc.vector.tensor_tensor(out=S[:], in0=S[:], in1=PS[bS][:, 0:256],
                                                                  op=ALU.add), r=[Sk, pk(bS)], w=[Sk])
                        op("act", lambda: nc.scalar.copy(out=Sbf[:], in_=S[:]), r=[Sk], w=[Sbk])
                        op("dve", lambda: nc.vector.tensor_tensor(
                            out=tz[kt][:], in0=PS[bY][:, 256:512].rearrange("p (h d) -> p h d", h=4),
                            in1=smallc[:, 128 + dr_ * 64 + c * 4:128 + dr_ * 64 + c * 4 + 4].unsqueeze(2).broadcast_to([128, 4, 64]), op=ALU.mult),
                           r=[pk(bY), "smallc"], w=[("tz", kt)])
                        if ci < NB // 2:
                            op("dve", lambda: nc.vector.tensor_tensor(
                                out=Y[:, c, :], in0=PS[bY][:, 0:256], in1=tz[kt][:].rearrange("p h d -> p (h d)"),
                                op=ALU.add), r=[pk(bY), ("tz", kt)], w=[("Y", c)])
                        else:
                            op("dve", lambda: nc.vector.tensor_tensor(
                                out=tz[kt][:].rearrange("p h d -> p (h d)"), in0=PS[bY][:, 0:256],
                                in1=tz[kt][:].rearrange("p h d -> p (h d)"), op=ALU.add),
                               r=[pk(bY), ("tz", kt)], w=[("tz", kt)])
                            op("pool", lambda: nc.gpsimd.tensor_tensor(
                                out=Y[:, c, :], in0=Y[:, c, :], in1=tz[kt][:].rearrange("p h d -> p (h d)"),
                                op=ALU.add), r=[("Y", c), ("tz", kt)], w=[("Y", c)])

                    prep(0, 0)
                    prep(0, 1)
                    for ci in range(NB):
                        if ci + 1 < NB:
                            prep(ci + 1, 0)
                            prep(ci + 1, 1)
                        main(ci, 0)
                        main(ci, 1)

                with scope() as sep:
                    t1 = [sb("ct1_%d" % i, [128, 4, 256], F32, sep) for i in range(2)]
                    sq = sb("csq", [128, 4, 256], F32, sep)
                    rs = sb("crs", [128, 8], F32, sep)
                    Yall = [("Y", c) for c in range(NB)]
                    for g4 in range(4):
                        k = g4 % 2
                        bs = slice(g4 * 4, g4 * 4 + 4)
                        op("pool", lambda: nc.gpsimd.tensor_tensor(
                            out=t1[k][:].rearrange("p b (h d) -> p (b h) d", h=4),
                            in0=Xtm[:, bs, :].rearrange("p b (h d) -> p (b h) d", h=4),
                            in1=dsk_bc.unsqueeze(1).broadcast_to([128, 4, 4]).rearrange("p b h -> p (b h)").unsqueeze(
                                2).broadcast_to([128, 16, 64]), op=ALU.mult) if False else nc.gpsimd.tensor_tensor(
                            out=t1[k][:].rearrange("p b (h d) -> p b h d", h=4),
                            in0=Xtm[:, bs, :].rearrange("p b (h d) -> p b h d", h=4),
                            in1=dsk_bc.unsqueeze(1).unsqueeze(3).broadcast_to([128, 4, 4, 64]), op=ALU.mult),
                           r=["Xtm", "rowsb"], w=[("ct1", k)])
                        op("pool", lambda: nc.gpsimd.tensor_tensor(out=t1[k][:], in0=t1[k][:], in1=Y[:, bs, :], op=ALU.add),
                           r=[("ct1", k)] + Yall, w=[("ct1", k)])
                        op("dve", lambda: nc.vector.tensor_tensor(out=t1[k][:], in0=t1[k][:], in1=Gc[:, bs, :],
                                                                  op=ALU.mult), r=[("ct1", k), "Gc"], w=[("ct1", k)])
                        op("dve", lambda: nc.vector.tensor_tensor(out=sq[:], in0=t1[k][:], in1=t1[k][:], op=ALU.mult),
                           r=[("ct1", k)], w=["csq"])
                        op("dve", lambda: nc.vector.tensor_reduce(out=rs[:], in_=sq[:].rearrange("p b (g n) -> p (b g) n", g=2),
                                                                  axis=AX.X, op=ALU.add), r=["csq"], w=["crs"])
                        rsqrt_col(rs[:], rs[:], 1.0 / 128, ["crs"], ["crs"])
                        op("dve", lambda: nc.vector.tensor_tensor(
                            out=t1[k][:].rearrange("p b (g n) -> p (b g) n", g=2),
                            in0=t1[k][:].rearrange("p b (g n) -> p (b g) n", g=2),
                            in1=rs[:].unsqueeze(2).broadcast_to([128, 8, 128]), op=ALU.mult),
                           r=[("ct1", k), "crs"], w=[("ct1", k)])
                        op("dve", lambda: nc.vector.tensor_tensor(
                            out=Ytm[:, bs, :], in0=t1[k][:], in1=snw_bc.unsqueeze(1).broadcast_to([128, 4, 256]),
                            op=ALU.mult), r=[("ct1", k), "rowsb"], w=["YtmC"])
                    dbg_dump(l, 2, Ytm, "YtmC", sep)
                    back_transpose(Ytm, "YtmC", 4)

        if dbg:
            with scope() as sd:
                tmp = sb("dbgh", [128, 2048], F32, sd)
                for c in range(8):
                    op("dve", lambda c=c: nc.vector.tensor_copy(out=tmp[:], in_=hT[:, c, :]), r=hT_all, w=["dbgh"])
                    dma("sp", dbg_h[:, c * T:(c + 1) * T], tmp[:], r=["dbgh"], w=["dbgout"])

        chk('norm0')
        for l in range(2):
            R0 = l * NROW
            alog_bc = rowsb[:, R0:R0 + 8]
            dtb_bc = rowsb[:, R0 + 8:R0 + 16]
            dsk_bc = rowsb[:, R0 + 16:R0 + 20]
            snw_bc = rowsb[:, R0 + 20:R0 + 276]
            dnw_bc = rowsb[:, R0 + 276:R0 + 532]
            lam_init = 0.8 - 0.6 * math.exp(-0.3 * l)

            for mixer in ("A", "B", "D"):
                with scope() as sm:
                    nk = 4 if mixer == "B" else 2
                    ncol = 1280 if mixer == "B" else 1024
                    wt = Wt
                    wkey = "W"
                    vcol = 512
                    QTt = sb("QT" + mixer, [128, 4, T], BF16, sm)
                    KTt = sb("KT" + mixer, [128, nk, T], BF16, sm)
                    Vt = sb("V" + mixer, [128, NB, 324], BF16, sm)
                    Gt = sb("G" + mixer, [128, NB, 256], BF16, sm)
                    Ytm = sb("Ytm" + mixer, [128, NB, 256], BF16, sm)
                    Pt = [sb("P%s%d" % (mixer, i), [128, 512], BF16, sm) for i in range(5)]
                    OTs = [sb("OTs%s%d" % (mixer, i), [65, 1024], F32, sm) for i in range(2)]
                    op("dve", lambda: nc.vector.memset(Vt[:], 1.0), w=["V"])
                    chk(mixer + str(l) + "w")
                    op("pool", lambda: nc.gpsimd.memset(QTt[:], 0.0), w=["QT"])

                    def evac_q(ps, tq, bank, cidx):
                        for hh in range(2):
                            copy_alt(QTt[hh * 64:(hh + 1) * 64, cidx * 2 + hh, tq * 512:(tq + 1) * 512],
                                     ps[hh * 64:(hh + 1) * 64, :], [pk(bank)], ["QT"])
                    for cidx in range(2):
                        proj_fm(wt, wkey, cidx * 128, lambda ps, tq, bank, cidx=cidx: evac_q(ps, tq, bank, cidx))
                    def evac_kb(ps, tq, bank, cidx):
                        op("dve", lambda: nc.vector.tensor_scalar(
                            out=KTt[:, cidx, tq * 512:(tq + 1) * 512], in0=ps[:, :], scalar1=kmask[0], scalar2=None,
                            op0=ALU.mult), r=[pk(bank), "cst"], w=["KT"])
                        op("act", lambda: nc.scalar.activation(
                            out=KTt[:, 2 + cidx, tq * 512:(tq + 1) * 512], in_=ps[:, :], func=AF.Copy, scale=kmask[1]),
                           r=[pk(bank), "cst"], w=["KT"])
                    for cidx in range(2):
                        if mixer == "B":
                            proj_fm(wt, wkey, 256 + cidx * 128, lambda ps, tq, bank, cidx=cidx: evac_kb(ps, tq, bank, cidx))
                        else:
                            proj_fm(wt, wkey, 256 + cidx * 128,
                                    lambda ps, tq, bank, cidx=cidx: copy_alt(KTt[:, cidx, tq * 512:(tq + 1) * 512],
                                                                             ps[:, :], [pk(bank)], ["KT"]))

                    chk(mixer + str(l) + "q")
                    import os
                    VAR = os.environ.get("KVAR", "")

                    def evac_vg(ps, blk, bank):
                        if VAR != "nov":
                          op("dve", lambda: nc.vector.tensor_copy(
                            out=Vt[:, blk, 0:260].rearrange("p (h d) -> p h d", d=65)[:, :, 0:64],
                            in_=ps[:, 0:256].rearrange("p (h d) -> p h d", d=64)), r=[pk(bank)], w=["V"])
                        if VAR != "nog":
                          op("act", lambda: nc.scalar.activation(out=Gt[:, blk, :], in_=ps[:, 256:512], func=AF.Silu),
                           r=[pk(bank)], w=["G"])
                    proj_tm(wt, wkey, vcol, 512, evac_vg)
                    if mixer == "B":
                        for blk in range(NB):
                            op("dve", lambda blk=blk: nc.vector.scalar_tensor_tensor(
                                out=Gt[:, blk, :], in0=Gt[:, blk, :], scalar=(1.0 - lam_init), in1=dnw_bc,
                                op0=ALU.mult, op1=ALU.mult), r=["G", "rowsb"], w=["G"])

                    rkeys = ["QT", "KT", "V"]
                    chk(mixer + str(l) + "p")
                    if mixer in ("A", "B"):
                        absd = sb("absd" + mixer, [128, TW], I16, sm)
                        tabs = [sb("tab%s%d" % (mixer, i), [128, TW], BF16, sm) for i in range(2)]
                        op("pool", lambda: nc.gpsimd.iota(absd[:], pattern=[[1, TW]], base=-C0, channel_multiplier=-1),
                           w=["absd"])
                        negd = tabs[1][:].bitcast(I16)
                        op("pool", lambda: nc.gpsimd.iota(negd, pattern=[[-1, TW]], base=C0, channel_multiplier=1),
                           w=[("tab", 1)])
                        op("dve", lambda: nc.vector.tensor_tensor(out=absd[:], in0=absd[:], in1=negd, op=ALU.max),
                           r=["absd", ("tab", 1)], w=["absd"])
                        if mixer == "A":
                            multA = sb("multA", [128, TW], BF16, sm)
                            for s_ in range(0, TW, 1984):
                                dma("pool", multA[:, s_:s_ + 1984], multA_d[:, s_:s_ + 1984], w=["multA"])
                    else:
                        EVM = [sb("EVM%d" % i, [128, 9 * 768], BF16, sm) for i in range(3)]

                        def load_evm(h_):
                            kk = h_ % 3
                            for q4 in range(4):
                                dma("pool", EVM[kk][:, q4 * 1728:(q4 + 1) * 1728],
                                    rpbm_d[l, h_, :, q4 * 1728:(q4 + 1) * 1728], w=[("EVM", kk)])
                        for h_ in range(3):
                            load_evm(h_)

                    chk(mixer + str(l) + "t")
                    otg = [sb("otg%s%d" % (mixer, g), [128, 4, 64], F32, sm) for g in range(2)]
                    rdg = [sb("rdg%s%d" % (mixer, g), [128, 4], F32, sm) for g in range(2)]
                    if mixer == "B":
                        O0 = sb("O0B", [128, NB, 64], F32, sm)
                        sqg = [sb("sqB%d" % g, [128, 4, 64], F32, sm) for g in range(2)]
                        rsg = [sb("rsB%d" % g, [128, 4], F32, sm) for g in range(2)]

                    def norm_group(psT, dst, g, dkey):
                        op("dve", lambda: nc.vector.reciprocal(out=rdg[g][:].unsqueeze(2), in_=psT[:, :, 64:65]),
                           r=[pk(7)], w=[("rdg", g)])
                        op("dve", lambda: nc.vector.tensor_tensor(out=dst, in0=psT[:, :, 0:64],
                                                                  in1=rdg[g][:].unsqueeze(2).broadcast_to([128, 4, 64]),
                                                                  op=ALU.mult), r=[pk(7), ("rdg", g)], w=[dkey])

                    def handlers(psT, blk0, g, h, i):
                        ok = ("otg", g)
                        ysl = Ytm[:, blk0:blk0 + 4, h * 64:(h + 1) * 64]
                        gsl = Gt[:, blk0:blk0 + 4, h * 64:(h + 1) * 64]
                        if mixer != "B":
                            def da():
                                norm_group(psT, otg[g][:], g, ok)
                                op("dve", lambda: nc.vector.tensor_tensor(out=ysl, in0=otg[g][:], in1=gsl, op=ALU.mult),
                                   r=[ok, "G"], w=["Ytm"])
                            return [da]
                        if i == 0:
                            def da():
                                norm_group(psT, O0[:, blk0:blk0 + 4, :], g, ("O0", blk0))
                            return [da]

                        def da():
                            norm_group(psT, otg[g][:], g, ok)
                            op("dve", lambda: nc.vector.scalar_tensor_tensor(
                                out=otg[g][:], in0=otg[g][:], scalar=nlam[:, l:l + 1], in1=O0[:, blk0:blk0 + 4, :],
                                op0=ALU.mult, op1=ALU.add), r=[ok, "nlam", ("O0", blk0)], w=[ok])
                            op("dve", lambda: nc.vector.tensor_tensor(out=sqg[g][:], in0=otg[g][:], in1=otg[g][:],
                                                                      op=ALU.mult), r=[ok], w=[("sqg", g)])
                            op("dve", lambda: nc.vector.tensor_reduce(out=rsg[g][:], in_=sqg[g][:], axis=AX.X, op=ALU.add),
                               r=[("sqg", g)], w=[("rsg", g)])
                            op("dve", lambda: nc.vector.tensor_scalar(out=rsg[g][:], in0=rsg[g][:], scalar1=1.0 / 64,
                                                                      scalar2=EPS, op0=ALU.mult, op1=ALU.add),
                               r=[("rsg", g)], w=[("rsg", g)])

                        def aa():
                            op("act", lambda: nc.scalar.activation(out=rsg[g][:], in_=rsg[g][:], func=AF.Ln),
                               r=[("rsg", g)], w=[("rsg", g)])
                            op("act", lambda: nc.scalar.activation(out=rsg[g][:], in_=rsg[g][:], func=AF.Exp, scale=-0.5),
                               r=[("rsg", g)], w=[("rsg", g)])

                        def db():
                            op("dve", lambda: nc.vector.tensor_tensor(
                                out=otg[g][:], in0=otg[g][:], in1=rsg[g][:].unsqueeze(2).broadcast_to([128, 4, 64]),
                                op=ALU.mult), r=[ok, ("rsg", g)], w=[ok])
                            op("dve", lambda: nc.vector.tensor_tensor(out=ysl, in0=otg[g][:], in1=gsl, op=ALU.mult),
                               r=[ok, "G"], w=["Ytm"])
                        return [da, aa, db]

                    epi = make_epilogue(OTs, handlers)

                    def bind_epi(h, i):
                        return lambda ob, qbase: epi(ob, qbase, h, i)

                    for h in range(4):
                        cidx, pb = h // 2, (h % 2) * 64
                        if mixer in ("A", "B"):
                            tb = tabs[h % 2]
                            tkey = ("tab", h % 2)
                            slope = slopes_a[h] if mixer == "A" else slopes_b[h]
                            if mixer == "A":
                                op("act", lambda: nc.scalar.activation(out=tb[:], in_=absd[:], func=AF.Exp, scale=-slope),
                                   r=["absd"], w=[tkey])
                                op("pool", lambda: nc.gpsimd.tensor_tensor(out=tb[:], in0=tb[:], in1=multA[:], op=ALU.mult),
                                   r=[tkey, "multA"], w=[tkey])
                            else:
                                op("act", lambda: nc.scalar.activation(out=tb[:], in_=absd[:], func=AF.Exp, scale=-slope),
                                   r=["absd"], w=[tkey])
                            mults = [("dve", lambda j, qa, qb, tb=tb: tb[:, qa - j * 128 + C0: qb - j * 128 + C0], [tkey])]
                        else:
                            k2 = h % 3
                            if h == 1:
                                load_evm(3)
                            op("act", lambda: nc.scalar.activation(out=EVM[k2][:], in_=EVM[k2][:], func=AF.Exp),
                               r=[("EVM", k2)], w=[("EVM", k2)])

                            def evf(j, qa, qb, k2=k2):
                                o = d_cls(j) * 768 + qa - d_ilo(j) * 128
                                return EVM[k2][:, o:o + (qb - qa)]
                            mults = [("dve", evf, [("EVM", k2)])]
                        if h == 0:
                            issue_next(l, mixer)
                        nsub = 2 if mixer == "B" else 1
                        for half in range(2):
                            for i in range(nsub):
                                if mixer == "B":
                                    KTf = lambda j, cidx=cidx, pb=pb, i=i: KTt[:, i * 2 + cidx, j * 128:(j + 1) * 128]
                                    sc_ = 32 ** -0.5
                                else:
                                    KTf = lambda j, cidx=cidx, pb=pb: KTt[:, cidx, j * 128:(j + 1) * 128]
                                    sc_ = 64 ** -0.5
                                QTf = lambda qa, qb, h=h: QTt[:, h, qa:qb]
                                Vf = lambda j, h=h: Vt[:, j, h * 65:h * 65 + 128]
                                steps = []
                                q0, q1 = half * 1024, (half + 1) * 1024
                                for j in range(NB):
                                    if mixer in ("A", "B"):
                                        kmax = int(math.floor(64.0 / slope / 128.0 - 1e-9)) + 1
                                        if mixer == "A":
                                            kmax = min(kmax, 8)
                                        lo, hi = max(0, (j - kmax) * 128), min(T, (j + kmax + 1) * 128)
                                    else:
                                        lo, hi = d_ilo(j) * 128, (d_ihi(j) + 1) * 128
                                    lo, hi = max(lo, q0), min(hi, q1)
                                    a = lo
                                    while a < hi:
                                        bnd = min(hi, (a // 512 + 1) * 512)
                                        steps.append((j, a, bnd))
                                        a = bnd
                                attention_task(KTf, QTf, Vf, steps, sc_, mults, rkeys, bind_epi(h, i), Pt)
                                chk(mixer + str(l) + "k")
                    while pending:
                        run_pending(1)
                    mi = {"A": 0, "B": 1, "D": 3}[mixer]
                    dbg_dump(l, mi, Ytm, "Ytm", sm)
                    back_transpose(Ytm, "Ytm", mi * 2)
                chk(mixer + str(l))

                if mixer == "B":
                    ssd(l)
                    chk('C' + str(l))

            with scope() as so:
                woutg = sb("woutg", [128, 8, D], BF16, so)
                wst = [sb("wst%d" % i, [128, D], F32, so) for i in range(2)]
                for c in range(8):
                    dma("sp", wst[c % 2][:], wout_d[l, c * 128:(c + 1) * 128, :], w=[("wst", c % 2)])
                    op("pool", lambda c=c: nc.gpsimd.tensor_tensor(out=woutg[:, c, :], in0=wst[c % 2][:],
                                                                   in1=gatebc[:, l, :], op=ALU.mult),
                       r=[("wst", c % 2), "gatebc"], w=["woutg"])
                xo = [sb("xo%d" % i, [128, D], F32, so) for i in range(4)]
                junk = [sb("junk%d" % i, [128, D], BF16, so) for i in range(2)]
                ssq = sb("ssq", [128, 4], F32, so)
                xn = [sb("xn%d" % i, [128, D], BF16, so) for i in range(8)]
                if l == 1:
                    fnw = sb("fnwbc", [128, D], F32, so)
                    dma("sp", fnw[:], fnw_d.partition_broadcast(128), w=["fnw"])
                    ot = [sb("ot%d" % i, [128, D], F32, so) for i in range(2)]
                src = x_d if l == 0 else x1_d

                def xload(blk):
                    dma("sp", xo[blk % 4][:], src[blk * 128:(blk + 1) * 128, :], r=["x1s"], w=[("xo", blk % 4)])

                def stage_a(blk):
                    k4 = blk % 4
                    if blk + 2 < NB:
                        xload(blk + 2)
                    for n in range(2):
                        bank = (blk * 2 + n) % 4
                        for c in range(8):
                            op("pe", lambda c=c, n=n, bank=bank: nc.tensor.matmul(
                                PS[bank][:, :], ycatT[:, c, blk * 128:(blk + 1) * 128], woutg[:, c, n * 512:(n + 1) * 512],
                                start=(c == 0), stop=(c == 7), skip_group_check=True),
                               r=[("ycatT", c, blk // 4), "woutg"], w=[pk(bank)], sig=(c == 7))
                        op("dve", lambda n=n, bank=bank: nc.vector.tensor_tensor(
                            out=xo[k4][:, n * 512:(n + 1) * 512], in0=PS[bank][:, :], in1=xo[k4][:, n * 512:(n + 1) * 512],
                            op=ALU.add), r=[pk(bank), ("xo", k4)], w=[("xo", k4)])
                    if l == 0:
                        dma("sp", x1_d[blk * 128:(blk + 1) * 128, :], xo[k4][:], r=[("xo", k4)], w=["x1s"])

                def stage_b(blk):
                    k4 = blk % 4
                    if l == 0:
                        k8 = blk % 8
                        norm_pre(xo[k4][:], ("xo", k4), ssq[:, k4:k4 + 1], ("ssq", k4), xn[k8][:], ("xn", k8),
                                 junk[blk % 2][:], ("junk", blk % 2))
                        if blk % 4 == 3:
                            b0 = blk - 3
                            norm_post4(1, [xn[(b0 + t) % 8] for t in range(4)], [("xn", (b0 + t) % 8) for t in range(4)],
                                       b0, [4, 5, 6, 7])
                    else:
                        k = blk % 2
                        op("act", lambda: nc.scalar.activation(out=junk[k][:], in_=xo[k4][:], func=AF.Square,
                                                               accum_out=ssq[:, k:k + 1]),
                           r=[("xo", k4)], w=[("junk", k), ("ssq", k)])
                        rsqrt_col(ssq[:, k:k + 1], ssq[:, k:k + 1], 1.0 / D, [("ssq", k)], [("ssq", k)])
                        op("dve", lambda: nc.vector.scalar_tensor_tensor(
                            out=ot[k][:], in0=xo[k4][:], scalar=ssq[:, k:k + 1], in1=fnw[:], op0=ALU.mult, op1=ALU.mult),
                           r=[("xo", k4), ("ssq", k), "fnw"], w=[("ot", k)])
                        dma("sp", out_d[blk * 128:(blk + 1) * 128, :], ot[k][:], r=[("ot", k)], w=["outd"])

                xload(0)
                xload(1)
                stage_a(0)
                for blk in range(NB):
                    if blk + 1 < NB:
                        stage_a(blk + 1)
                    stage_b(blk)


def d_ilo(j):
    if j == 3:
        return 0
    return max(0, j - 2)


def d_cls(j):
    if 4 <= j <= 11:
        return 0
    return j + 1 if j < 4 else j - 7


def d_ihi(j):
    if j == 12:
        return 15
    return min(15, j + 2)


def _consts():
    m = np.arange(128)[:, None]
    n = np.arange(128)[None, :]
    ka = ((np.arange(128) % 64) < 32).astype(np.float64)[:, None]
    cst = np.concatenate([np.eye(128), (m <= n), (m > n), (m < n), (m >= n), np.ones((128, 128)), ka, 1.0 - ka], axis=1)
    cst = cst.astype(np.float32)
    d = (np.arange(TW)[None, :] - C0 - np.arange(128)[:, None])
    ad = np.abs(d)
    mult = (ad <= 64).astype(np.float32) + ((d % 4 == 0) & (ad <= 256)) + ((d % 16 == 0) & (ad <= 1024))
    multA = mult.astype(np.float32)
    kr = np.arange(128)
    b_, ck = kr // 64, kr % 64
    qr = np.arange(128)
    a_, cq = qr // 64, qr % 64
    cs = np.clip(cq - 8, 0, 48)
    colok = (ck[:, None] >= cs[None, :]) & (ck[:, None] <= cs[None, :] + 15)
    drr = np.zeros((128, 9, 6, 128), np.int64)
    dcc = np.zeros((128, 9, 6, 128), np.int64)
    val = np.zeros((128, 9, 6, 128), bool)
    reps = {0: 6}
    for j in range(16):
        reps.setdefault(d_cls(j), j)
    for j in sorted(reps.values()) + list(range(16)):
        cls = d_cls(j)
        for i in range(16):
            rq = 2 * i + a_
            rk = 2 * j + b_
            rs = np.clip(rq - 4, 0, 24)
            rowok = (rk[:, None] >= rs[None, :]) & (rk[:, None] <= rs[None, :] + 7)
            v = rowok & colok
            if d_ilo(j) <= i <= d_ihi(j):
                e = i - d_ilo(j)
                dr_ = np.clip(rk[:, None] - rq[None, :] + 7, 0, 14)
                dc_ = np.clip(ck[:, None] - cq[None, :] + 15, 0, 30)
                if reps[cls] == j:
                    drr[:, cls, e, :] = dr_
                    dcc[:, cls, e, :] = dc_
                    val[:, cls, e, :] = v
                else:
                    assert (val[:, cls, e, :] == v).all() and (drr[:, cls, e, :][v] == dr_[v]).all()
            else:
                assert not v.any(), (i, j)
    return cst, multA, drr, dcc, val


_CACHE = {}
_STOP = None
_NCORES = 8


def kernel(x, c, norm_w, ada_w, ada_b, w_in, diff_lambda, diff_norm_w, conv_w, conv_b, ssm_a_log, ssm_dt_bias,
           ssm_d, ssm_norm_w, na_rpb, w_out, final_norm_w, _dbg=False):
    f = lambda a: np.ascontiguousarray(np.asarray(a, dtype=np.float32))
    x, c, norm_w, ada_w, ada_b, w_in = f(x), f(c), f(norm_w), f(ada_w), f(ada_b), f(w_in)
    cst, multA, drr, dcc, val = _consts()
    rpb = f(na_rpb)
    rpbm = f(np.where(val[None, None], rpb[:, :, drr, dcc], np.float32(-30000.0)).reshape(2, 4, 128, 9 * 768))
    rows = []
    for l in range(2):
        rows.append(np.concatenate([f(ssm_a_log)[l].reshape(-1), f(ssm_dt_bias)[l].reshape(-1), f(ssm_d)[l].reshape(-1),
                                    f(ssm_norm_w)[l].reshape(-1), np.tile(f(diff_norm_w)[l].reshape(-1), 4)]))
    rows = f(np.concatenate(rows).reshape(1, -1))
    nw_col = f(norm_w.reshape(2, 8, 128).transpose(2, 0, 1).reshape(128, 16))
    cw_col = f(f(conv_w).reshape(2, 5, 6, 128).transpose(3, 0, 1, 2).reshape(128, 60))
    cb_col = f(f(conv_b).reshape(2, 6, 128).transpose(2, 0, 1).reshape(128, 12))
    ada_w = f(ada_w.reshape(2, 8, 128, 24, 128).transpose(0, 3, 2, 1, 4).reshape(2, 24, 128, 1024))
    shared = {"cst": cst, "ada_w": ada_w, "ada_b": f(ada_b.reshape(1, -1)), "nw_col": nw_col, "w_in": w_in,
              "w_out": f(w_out), "fnw": f(final_norm_w).reshape(1, -1), "rows": rows,
              "dlam": f(diff_lambda).reshape(1, -1), "cw_col": cw_col, "cb_col": cb_col, "multA": multA,
              "rpbm": rpbm}
    in_maps = []
    for b in range(_NCORES):
        m = dict(shared)
        m["x"] = x[b]
        m["ccol"] = f(c[b].reshape(8, 128).T)
        in_maps.append(m)
    key = bool(_dbg)
    if key not in _CACHE:
        _CACHE[key] = build_program(dbg=_dbg, stop=_STOP)
    nc = _CACHE[key]
    res = run_bass_kernel_spmd(nc, in_maps, core_ids=list(range(_NCORES)))
    out = np.stack([np.asarray(r["out"], dtype=np.float32) for r in res.results], axis=0)
    if _dbg:
        return out, res.results
    return out
```

```python
import math
import numpy as np
import concourse.bass as bass
import concourse.mybir as mybir
from concourse.bass_utils import run_bass_kernel_spmd
from contextlib import ExitStack

F32 = mybir.dt.float32
BF16 = mybir.dt.bfloat16
I16 = mybir.dt.int16
AF = mybir.ActivationFunctionType
ALU = mybir.AluOpType
AX = mybir.AxisListType

T = 2048
D = 1024
NB = 16
DIN = 4104
EPS = 1e-6
TW = 3968
C0 = 1920
NROW = 8 + 8 + 4 + 256 + 256


class _Own:
    def __init__(self, sem, name):
        self.sem = sem
        self.count = 0
        self.name = name
        self.seen = {}
        self.last = None
        self.h = None


class KB:
    def __init__(self, nc, es):
        self.nc = nc
        self.es = es
        self.eng = {}
        for name, h in (("pe", nc.tensor), ("act", nc.scalar), ("dve", nc.vector),
                        ("pool", nc.gpsimd), ("sp", nc.sync)):
            o = _Own(es.enter_context(nc.semaphore("p_" + name)), name)
            o.h = h
            self.eng[name] = o
        self.dsems = [_Own(es.enter_context(nc.semaphore("dq%d" % i)), "dq%d" % i) for i in range(48)]
        self.di = 0
        self.bufs = {}
        self.stopped = False
        self.consumed = set()

    def _deps(self, reads, writes):
        deps = []
        for b in reads:
            st = self.bufs.get(b)
            if st and st[0] is not None:
                deps.append(st[0])
        for b in writes:
            st = self.bufs.get(b)
            if st:
                if st[0] is not None:
                    deps.append(st[0])
                deps.extend(st[1])
        return deps

    def _wait(self, e, tok):
        own, val = tok
        if e.seen.get(own, 0) >= val:
            return
        e.h.wait_ge(own.sem, val)
        e.seen[own] = val
        self.consumed.add((id(own), val))

    def _record(self, tok, reads, writes):
        for b in reads:
            st = self.bufs.setdefault(b, [None, []])
            st[1].append(tok)
        for b in writes:
            self.bufs[b] = [tok, []]

    def barrier(self):
        if self.stopped:
            return
        toks = [(e, e.count) for e in self.eng.values() if e.count > 0]
        toks += [d.last for d in self.dsems if d.last is not None and (id(d.last[0]), d.last[1]) not in self.consumed]
        for e in self.eng.values():
            for tok in toks:
                if tok[0] is not e:
                    self._wait(e, tok)

    def op(self, en, fn, r=(), w=(), sig=True):
        if self.stopped:
            return None
        e = self.eng[en]
        pr = [k for k in r if isinstance(k, tuple) and k[0] == "ps"]
        if pr:
            w = list(w) + [k for k in pr if k not in w]
            r = [k for k in r if k not in pr]
        for tok in self._deps(r, w):
            if tok[0] is e and en == "pe":
                continue
            self._wait(e, tok)
        inst = fn()
        if sig:
            e.count += 1
            inst.then_inc(e.sem, 1)
            tok = (e, e.count)
        else:
            tok = (e, e.count + 1)
        self._record(tok, r, w)
        return tok

    def dma(self, qn, out, in_, r=(), w=(), **kw):
        if self.stopped:
            return None
        e = self.eng[qn]
        d = self.dsems[self.di % len(self.dsems)]
        self.di += 1
        if d.last is not None:
            self._wait(e, d.last)
        for tok in self._deps(r, w):
            self._wait(e, tok)
        inst = e.h.dma_start(out=out, in_=in_, **kw)
        d.count += 16
        inst.then_inc(d.sem, 16)
        tok = (d, d.count)
        d.last = tok
        self._record(tok, r, w)
        return tok


class _Stop(Exception):
    pass


def build_program(dbg=False, stop=None):
    nc = bass.Bass("TRN2", target_bir_lowering=False)
    dr = {}

    def din(name, shape):
        dr[name] = nc.dram_tensor(name, list(shape), F32, kind="ExternalInput")
        return dr[name].ap()

    x_d = din("x", [T, D])
    ccol_d = din("ccol", [128, 8])
    cst_d = din("cst", [128, 770])
    adaw_d = din("ada_w", [2, 24, 128, 1024])
    adab_d = din("ada_b", [1, 2 * 3 * D])
    nwcol_d = din("nw_col", [128, 16])
    win_d = din("w_in", [2, D, DIN])
    wout_d = din("w_out", [2, D, D])
    fnw_d = din("fnw", [1, D])
    rows_d = din("rows", [1, 2 * NROW])
    dlam_d = din("dlam", [1, 256])
    cwcol_d = din("cw_col", [128, 60])
    cbcol_d = din("cb_col", [128, 12])
    multA_d = din("multA", [128, TW])
    rpbm_d = din("rpbm", [2, 4, 128, 9 * 768])
    out_d = nc.dram_tensor("out", [T, D], F32, kind="ExternalOutput").ap()
    x1_d = nc.dram_tensor("x1s", [T, D], F32, kind="Internal").ap()
    if dbg:
        dbg_ycat = nc.dram_tensor("dbg_ycat", [2, 4, T, 256], F32, kind="ExternalOutput").ap()
        dbg_h = nc.dram_tensor("dbg_h", [128, 8 * T], F32, kind="ExternalOutput").ap()

    slopes_all = [2.0 ** (-8.0 * i / 8) for i in range(1, 9)]
    slopes_a = slopes_all[0::2]
    slopes_b = slopes_all[1::2]

    with ExitStack() as es:
        kb = KB(nc, es)
        op, dma = kb.op, kb.dma

        def chk(name):
            if stop == name:
                kb.stopped = True
        try:
            _body(nc, es, kb, dbg, chk, locals())
        except _Stop:
            pass
        sp = kb.eng["sp"]
        for d in kb.dsems:
            if d.last is not None:
                kb._wait(sp, d.last)
    return nc


def _body(nc, es, kb, dbg, chk, env):
    globals_ = env
    (x_d, ccol_d, cst_d, adaw_d, adab_d, nwcol_d, win_d, wout_d, fnw_d, rows_d, dlam_d, cwcol_d, cbcol_d, multA_d, rpbm_d,
     out_d, x1_d, slopes_a, slopes_b) = [env[k] for k in (
        "x_d", "ccol_d", "cst_d", "adaw_d", "adab_d", "nwcol_d", "win_d", "wout_d", "fnw_d", "rows_d", "dlam_d", "cwcol_d",
        "cbcol_d", "multA_d", "rpbm_d", "out_d", "x1_d", "slopes_a", "slopes_b")]
    dbg_ycat = env.get("dbg_ycat")
    dbg_h = env.get("dbg_h")
    op, dma = kb.op, kb.dma

    class scope(ExitStack):
        def __exit__(self, *a):
            kb.barrier()
            return ExitStack.__exit__(self, *a)
    if True:

        uid = [0]

        def sb(name, shape, dt, stack=None):
            uid[0] += 1
            return (stack or es).enter_context(nc.sbuf_tensor("s%d_%s" % (uid[0], name), list(shape), dt))

        PS = [es.enter_context(nc.psum_tensor("ps%d" % i, [128, 512], F32)) for i in range(8)]

        def pk(i):
            return ("ps", i)

        hT = sb("hT", [128, 8, T], BF16)
        ycatT = sb("ycatT", [128, 8, T], BF16)
        cst = sb("cst", [128, 770], F32)
        identb = sb("identb", [128, 128], BF16)
        rowsb = sb("rowsb", [128, 2 * NROW], F32)
        nwcol = sb("nwcol", [128, 16], F32)
        Acol = sb("Acol", [128, 16], F32)
        Bcol = sb("Bcol", [128, 16], F32)
        gatebc = sb("gatebc", [128, 2, D], F32)
        sccol = sb("sccol", [128, 8], BF16)
        Wt = sb("Wt", [128, 8, 1032], BF16)
        nlam = sb("nlam", [128, 2], F32)
        small = sb("small", [128, 64], F32)
        identf = cst[:, 0:128]
        U_le = cst[:, 128:256]
        U_gt = cst[:, 256:384]
        U_lt = cst[:, 384:512]
        U_ge = cst[:, 512:640]
        onesf = cst[:, 640:768]
        kmask = [cst[:, 768:769], cst[:, 769:770]]

        dma("sp", cst[:], cst_d, w=["cst"])
        dma("sp", nwcol[:], nwcol_d, w=["nwcol"])
        dma("sp", rowsb[:], rows_d.partition_broadcast(128), w=["rowsb"])
        op("dve", lambda: nc.vector.tensor_copy(out=identb[:], in_=identf), r=["cst"], w=["identb"])

        def load_w(wt, key, l, segs):
            for (dc, sc, n) in segs:
                src = win_d[l, :, sc:sc + n].rearrange("(c p) n -> p c n", p=128)
                dma("pool", wt[:, :, dc:dc + n], src, w=[key])

        MIX_ORDER = [(l_, m_) for l_ in range(2) for m_ in ("A", "B", "C", "D")]

        def issue_weights(l_, m_):
            if m_ == "C":
                load_w(Wt, "W", l_, [(0, 2048, 256), (256, 3072, 8), (264, 2304, 768)])
            else:
                base_ = {"A": 0, "B": 1024, "D": 3080}[m_]
                load_w(Wt, "W", l_, [(0, base_, 512), (512, base_ + 512, 512)])

        def issue_next(l_, m_):
            i_ = MIX_ORDER.index((l_, m_))
            if i_ + 1 < len(MIX_ORDER):
                issue_weights(*MIX_ORDER[i_ + 1])

        def rsqrt_col(dst, src, scale, tagr, tagw):
            op("dve", lambda: nc.vector.tensor_scalar(out=dst, in0=src, scalar1=scale, scalar2=EPS, op0=ALU.mult,
                                                      op1=ALU.add), r=tagr, w=tagw)
            op("act", lambda: nc.scalar.activation(out=dst, in_=dst, func=AF.Ln), r=tagw, w=tagw)
            op("act", lambda: nc.scalar.activation(out=dst, in_=dst, func=AF.Exp, scale=-0.5), r=tagw, w=tagw)

        def norm_pre(xin, xkey, ssc, sskey, xn_ap, xnkey, junk_ap, jkey):
            op("act", lambda: nc.scalar.activation(out=junk_ap, in_=xin, func=AF.Square, accum_out=ssc),
               r=[xkey], w=[jkey, sskey])
            rsqrt_col(ssc, ssc, 1.0 / D, [sskey], [sskey])
            op("dve", lambda: nc.vector.tensor_scalar(out=xn_ap, in0=xin, scalar1=ssc, scalar2=None, op0=ALU.mult),
               r=[xkey, sskey], w=[xnkey])

        def norm_post(l, xn_ap, xnkey, blk, bank):
            psb = PS[bank][:].bitcast(BF16)
            for c in range(8):
                op("pe", lambda c=c: nc.tensor.transpose(psb[:, c * 128:(c + 1) * 128], xn_ap[:, c * 128:(c + 1) * 128],
                                                         identb[:]), r=[xnkey, "identb"], w=[pk(bank)], sig=(c == 7))
            for c in range(8):
                if (c + blk) % 2 == 0:
                    op("dve", lambda c=c: nc.vector.tensor_scalar(
                        out=hT[:, c, blk * 128:(blk + 1) * 128], in0=psb[:, c * 128:(c + 1) * 128],
                        scalar1=Acol[:, l * 8 + c:l * 8 + c + 1], scalar2=Bcol[:, l * 8 + c:l * 8 + c + 1],
                        op0=ALU.mult, op1=ALU.add), r=[pk(bank), "Acol", "Bcol"], w=[("hT", blk, 0)])
                else:
                    op("act", lambda c=c: nc.scalar.activation(
                        out=hT[:, c, blk * 128:(blk + 1) * 128], in_=psb[:, c * 128:(c + 1) * 128], func=AF.Identity,
                        scale=Acol[:, l * 8 + c:l * 8 + c + 1], bias=Bcol[:, l * 8 + c:l * 8 + c + 1]),
                       r=[pk(bank), "Acol", "Bcol"], w=[("hT", blk, 1)])

        def norm_post4(l, xn_list, xnkeys, blk0, banks):
            for c in range(8):
                bank = banks[c % len(banks)]
                psb = PS[bank][:].bitcast(BF16)
                for t in range(4):
                    op("pe", lambda t=t, c=c, psb=psb: nc.tensor.transpose(
                        psb[:, t * 128:(t + 1) * 128], xn_list[t][:, c * 128:(c + 1) * 128], identb[:]),
                       r=[xnkeys[t], "identb"], w=[pk(bank)], sig=(t == 3))
                dst = hT[:, c, blk0 * 128:(blk0 + 4) * 128]
                wk = [("hT", blk0 + t, c % 2) for t in range(4)]
                if c % 2 == 0:
                    op("dve", lambda c=c, psb=psb, dst=dst: nc.vector.tensor_scalar(
                        out=dst, in0=psb[:, 0:512], scalar1=Acol[:, l * 8 + c:l * 8 + c + 1],
                        scalar2=Bcol[:, l * 8 + c:l * 8 + c + 1], op0=ALU.mult, op1=ALU.add),
                       r=[pk(bank), "Acol", "Bcol"], w=wk)
                else:
                    op("act", lambda c=c, psb=psb, dst=dst: nc.scalar.activation(
                        out=dst, in_=psb[:, 0:512], func=AF.Identity, scale=Acol[:, l * 8 + c:l * 8 + c + 1],
                        bias=Bcol[:, l * 8 + c:l * 8 + c + 1]), r=[pk(bank), "Acol", "Bcol"], w=wk)

        hT_all = [("hT", b, p_) for b in range(NB) for p_ in range(2)]

        def mod_gen(l, stk, pbanks=(4, 5), cbanks=(6, 7), dve_cast=False):
            NBF = 4
            adab3 = [sb("adab3_%d" % i, [1, D], F32, stk) for i in range(2)]
            modrow3 = sb("modrow3", [1, D], F32, stk)
            wp = [sb("wp%d" % i, [128, 8, 128], F32, stk) for i in range(NBF)]
            wpb = [sb("wpb%d" % i, [128, 8, 128], BF16, stk) for i in range(NBF)]

            def stage_a(p):
                third, pc = divmod(p, 8)
                k = p % NBF
                if pc == 0:
                    dma("sp", adab3[third % 2][:], adab_d[:, l * 3 * D + third * D: l * 3 * D + (third + 1) * D],
                        w=[("adab3", third % 2)])
                dma("sp", wp[k][:].rearrange("p c n -> p (c n)"), adaw_d[l, p], w=[("wp", k)])
                if dve_cast and p % 2 == 1:
                    op("dve", lambda: nc.vector.tensor_copy(out=wpb[k][:], in_=wp[k][:]), r=[("wp", k)],
                       w=[("wpb", k)])
                else:
                    op("pool", lambda: nc.gpsimd.tensor_copy(out=wpb[k][:], in_=wp[k][:]), r=[("wp", k)],
                       w=[("wpb", k)])
            for p0 in range(3):
                stage_a(p0)
            for third in range(3):
                for pc in range(8):
                    p = third * 8 + pc
                    if p + 3 < 24:
                        stage_a(p + 3)
                    k = p % NBF
                    bank = pbanks[p % 2]
                    for c in range(8):
                        op("pe", lambda c=c, k=k, bank=bank: nc.tensor.matmul(
                            PS[bank][0:1, 0:128], sccol[:, c:c + 1], wpb[k][:, c, :], start=(c == 0), stop=(c == 7),
                            skip_group_check=True),
                           r=["sccol", ("wpb", k)], w=[pk(bank)], sig=(c == 7))
                    op("dve", lambda pc=pc, bank=bank, third=third: nc.vector.tensor_tensor(
                        out=modrow3[0:1, pc * 128:(pc + 1) * 128], in0=PS[bank][0:1, 0:128],
                        in1=adab3[third % 2][0:1, pc * 128:(pc + 1) * 128], op=ALU.add),
                       r=[pk(bank), ("adab3", third % 2)], w=["modrow3"])
                    if pc < 7:
                        yield (third, pc)
                if third < 2:
                    for k8 in range(8):
                        op("pe", lambda k8=k8: nc.tensor.matmul(
                            PS[cbanks[0]][:, k8:k8 + 1], modrow3[0:1, k8 * 128:(k8 + 1) * 128], onesf[0:1, 0:1],
                            start=(k8 == 0), stop=(k8 == 7), skip_group_check=True),
                           r=["modrow3", "cst"], w=[pk(cbanks[0])], sig=(k8 == 7))
                    if third == 0:
                        op("dve", lambda: nc.vector.tensor_copy(out=Bcol[:, l * 8:(l + 1) * 8], in_=PS[cbanks[0]][:, 0:8]),
                           r=[pk(cbanks[0])], w=["Bcol"])
                    else:
                        op("dve", lambda: nc.vector.scalar_tensor_tensor(
                            out=Acol[:, l * 8:(l + 1) * 8], in0=PS[cbanks[0]][:, 0:8], scalar=1.0,
                            in1=nwcol[:, l * 8:(l + 1) * 8], op0=ALU.add, op1=ALU.mult),
                           r=[pk(cbanks[0]), "nwcol"], w=["Acol"])
                else:
                    for n in range(2):
                        op("pe", lambda n=n: nc.tensor.matmul(
                            PS[cbanks[1]][:, :], onesf[0:1, :], modrow3[0:1, n * 512:(n + 1) * 512],
                            start=True, stop=True, skip_group_check=True), r=["modrow3", "cst"], w=[pk(cbanks[1])])
                        op("dve", lambda n=n: nc.vector.tensor_copy(out=gatebc[:, l, n * 512:(n + 1) * 512],
                                                                     in_=PS[cbanks[1]][:, :]), r=[pk(cbanks[1])], w=["gatebc"])
                yield (third, 7)

        with scope() as s0:
            ccol = sb("ccol", [128, 8], F32, s0)
            dlam = sb("dlam", [1, 256], F32, s0)
            ltmp = sb("ltmp", [1, 136], F32, s0)
            xn_all = sb("xn_all", [128, NB, D], BF16, s0)
            junk0 = [sb("junk0_%d" % i, [128, D], BF16, s0) for i in range(2)]
            ssq0 = sb("ssq0", [128, NB], F32, s0)
            xin = [sb("xin%d" % i, [128, D], F32, s0) for i in range(3)]
            pre_blocks = list(range(NB))

            def xload0(blk):
                dma("act", xin[blk % 3][:], x_d[blk * 128:(blk + 1) * 128, :], w=[("xin", blk % 3)])
            xload0(0)
            xload0(1)

            def pre_norm_some(n):
                for _ in range(n):
                    if not pre_blocks:
                        return
                    blk = pre_blocks.pop(0)
                    k3 = blk % 3
                    if blk + 2 < NB:
                        xload0(blk + 2)
                    norm_pre(xin[k3][:], ("xin", k3), ssq0[:, blk:blk + 1], ("ssq0", blk), xn_all[:, blk, :],
                             ("xn_all", blk), junk0[blk % 2][:], ("junk0", blk % 2))
            dma("sp", ccol[:], ccol_d, w=["ccol"])
            dma("sp", dlam[:], dlam_d, w=["dlam"])
            op("act", lambda: nc.scalar.activation(out=sccol[:], in_=ccol[:], func=AF.Silu), r=["ccol"], w=["sccol"])
            for (third, pc) in mod_gen(0, s0, dve_cast=True):
                if third == 1 and pc == 0:
                    issue_weights(0, "A")
                if third < 2:
                    pre_norm_some(1)
                elif pc % 2 == 0:
                    g_ = pc // 2
                    pre_norm_some(NB)
                    norm_post4(0, [xn_all[:, g_ * 4 + t, :] for t in range(4)],
                               [("xn_all", g_ * 4 + t) for t in range(4)], g_ * 4, [0, 1, 2, 3])
            for l in range(2):
                lam_init = 0.8 - 0.6 * math.exp(-0.3 * l)
                dl = dlam[0:1, l * 128:(l + 1) * 128].rearrange("p (a t b) -> p a t b", a=2, t=2)
                op("dve", lambda dl=dl: nc.vector.tensor_tensor(
                    out=ltmp[0:1, 0:64].rearrange("p (a b) -> p a b", a=2), in0=dl[:, :, 0, :], in1=dl[:, :, 1, :],
                    op=ALU.mult), r=["dlam"], w=["ltmp"])
                op("dve", lambda: nc.vector.tensor_reduce(
                    out=ltmp[0:1, 64:66], in_=ltmp[0:1, 0:64].rearrange("p (a b) -> p a b", a=2), axis=AX.X,
                    op=ALU.add), r=["ltmp"], w=["ltmp"])
                op("act", lambda: nc.scalar.activation(out=ltmp[0:1, 66:68], in_=ltmp[0:1, 64:66], func=AF.Exp),
                   r=["ltmp"], w=["ltmp"])
                op("dve", lambda lam_init=lam_init: nc.vector.scalar_tensor_tensor(
                    out=ltmp[0:1, 68:69], in0=ltmp[0:1, 67:68], scalar=-lam_init, in1=ltmp[0:1, 66:67],
                    op0=ALU.add, op1=ALU.subtract), r=["ltmp"], w=["ltmp"])
                op("pe", lambda: nc.tensor.matmul(PS[7][:, 0:1], onesf[0:1, :], ltmp[0:1, 68:69], start=True,
                                                  stop=True, skip_group_check=True), r=["ltmp", "cst"], w=[pk(7)])
                op("dve", lambda l=l: nc.vector.tensor_copy(out=nlam[:, l:l + 1], in_=PS[7][:, 0:1]),
                   r=[pk(7)], w=["nlam"])

        chk('startup')

        proj_rot = [0]
        proj_banks = [[0, 1, 2, 3, 4, 5, 6]]

        def proj_fm(wt, wkey, col0, evac):
            for tq in range(4):
                bank = proj_banks[0][proj_rot[0] % len(proj_banks[0])]
                proj_rot[0] += 1
                for c in range(8):
                    op("pe", lambda c=c, bank=bank: nc.tensor.matmul(
                        PS[bank][:, :], wt[:, c, col0:col0 + 128], hT[:, c, tq * 512:(tq + 1) * 512],
                        start=(c == 0), stop=(c == 7), skip_group_check=True),
                       r=[wkey] + hT_all[tq * 8:(tq + 1) * 8], w=[pk(bank)], sig=(c == 7))
                evac(PS[bank], tq, bank)

        def proj_tm(wt, wkey, col0, n, evac):
            for blk in range(NB):
                bank = proj_banks[0][proj_rot[0] % len(proj_banks[0])]
                proj_rot[0] += 1
                for c in range(8):
                    op("pe", lambda c=c, bank=bank: nc.tensor.matmul(
                        PS[bank][:, 0:n], hT[:, c, blk * 128:(blk + 1) * 128], wt[:, c, col0:col0 + n],
                        start=(c == 0), stop=(c == 7), skip_group_check=True),
                       r=[wkey, ("hT", blk, 0), ("hT", blk, 1)], w=[pk(bank)], sig=(c == 7))
                evac(PS[bank], blk, bank)

        alt = [0]

        def copy_alt(out, in_, r, w):
            alt[0] += 1
            if alt[0] % 2:
                op("dve", lambda: nc.vector.tensor_copy(out=out, in_=in_), r=r, w=w)
            else:
                op("act", lambda: nc.scalar.copy(out=out, in_=in_), r=r, w=w)

        pending = []

        def run_pending(n=1):
            for _ in range(n):
                if pending:
                    pending.pop(0)()

        orot = [0]
        srot = [0]

        def attention_task(KT, QT, V, steps, scale, mults, rkeys, epilogue, Pt):
            qbase = (steps[0][1] // 1024) * 1024
            ob = [0, 1]
            orot[0] += 1
            started = set()
            nsteps = len(steps)
            last_in_bank = {}
            for si, (j, qa, qb) in enumerate(steps):
                last_in_bank[(qa - qbase) // 512] = si
            sinfo = {}

            def emit_S(si):
                j, qa, qb = steps[si]
                n = qb - qa
                sbank = 2 + srot[0] % 5
                pidx = srot[0] % 5
                srot[0] += 1
                op("pe", lambda: nc.tensor.matmul(PS[sbank][:, 0:n], KT(j), QT(qa, qb), start=True, stop=True,
                                                  skip_group_check=True), r=rkeys, w=[pk(sbank)])
                op("act", lambda: nc.scalar.activation(out=Pt[pidx][:, 0:n], in_=PS[sbank][:, 0:n], func=AF.Exp,
                                                       scale=scale), r=[pk(sbank)], w=[("P", pidx)])
                for (en, fn, keys) in mults:
                    ap = fn(j, qa, qb)
                    if en == "dve":
                        op("dve", lambda ap=ap: nc.vector.tensor_tensor(out=Pt[pidx][:, 0:n], in0=Pt[pidx][:, 0:n],
                                                                        in1=ap, op=ALU.mult),
                           r=[("P", pidx)] + keys, w=[("P", pidx)])
                    else:
                        op("pool", lambda ap=ap: nc.gpsimd.tensor_tensor(out=Pt[pidx][:, 0:n], in0=Pt[pidx][:, 0:n],
                                                                         in1=ap, op=ALU.mult),
                           r=[("P", pidx)] + keys, w=[("P", pidx)])
                sinfo[si] = pidx

            def emit_PV(si):
                j, qa, qb = steps[si]
                n = qb - qa
                b = (qa - qbase) // 512
                bank = ob[b]
                o0 = qa - qbase - b * 512
                first = b not in started
                started.add(b)
                pidx = sinfo[si]
                op("pe", lambda: nc.tensor.matmul(PS[bank][:, o0:o0 + n], V(j), Pt[pidx][:, 0:n], start=first,
                                                  stop=(last_in_bank[b] == si), skip_group_check=True),
                   r=[("P", pidx)] + rkeys, w=[pk(bank)], sig=(last_in_bank[b] == si))

            LA = 4
            for si in range(min(LA, nsteps)):
                emit_S(si)
            for si in range(nsteps):
                if si + LA < nsteps:
                    emit_S(si + LA)
                emit_PV(si)
                run_pending(1)
            epilogue(ob, qbase)

        def make_epilogue(OTs, handlers):
            erot = [0]

            def epilogue(ob, qbase, h, i):
                while pending:
                    run_pending(1)
                k = erot[0] % 2
                erot[0] += 1
                for b in range(2):
                    op("dve", lambda b=b: nc.vector.tensor_copy(out=OTs[k][0:65, b * 512:(b + 1) * 512],
                                                                in_=PS[ob[b]][0:65, :]),
                       r=[pk(ob[b])], w=[("OTs", k)])

                def tstage(g):
                    def f():
                        for t in range(4):
                            blk = g * 4 + t
                            op("pe", lambda t=t, blk=blk: nc.tensor.transpose(
                                PS[7][:, t * 65:(t + 1) * 65], OTs[k][0:65, blk * 128:(blk + 1) * 128],
                                identf[0:65, 0:65]), r=[("OTs", k), "cst"], w=[pk(7)], sig=(t == 3))
                    return f
                psT = PS[7][:, 0:260].rearrange("p (g d) -> p g d", d=65)
                hs = [handlers(psT, qbase // 128 + g * 4, g, h, i) for g in range(2)]
                if len(hs[0]) == 1:
                    seq = [tstage(0), hs[0][0], tstage(1), hs[1][0]]
                else:
                    seq = [tstage(0), hs[0][0], tstage(1), hs[0][1], hs[1][0], hs[0][2], hs[1][1], hs[1][2]]
                pending.extend(seq)
            return epilogue

        def back_transpose(Ytm, ykey, chunk0, nchunks=2):
            for cc in range(nchunks):
                for g in range(4):
                    bank = 4 + g
                    psb = PS[bank][:].bitcast(BF16)
                    for t in range(4):
                        blk = g * 4 + t
                        op("pe", lambda t=t, blk=blk, psb=psb: nc.tensor.transpose(
                            psb[:, t * 128:(t + 1) * 128], Ytm[:, blk, cc * 128:(cc + 1) * 128], identb[:]),
                           r=[ykey, "identb"], w=[pk(bank)], sig=(t == 3))
                    copy_alt(ycatT[:, chunk0 + cc, g * 512:(g + 1) * 512], psb[:, 0:512], [pk(bank)],
                             [("ycatT", chunk0 + cc, g)])

        def dbg_dump(l, m, Ytm, ykey, stk):
            if not dbg:
                return
            tmp = sb("dbgtmp%d%d" % (l, m), [128, 4, 256], F32, stk)
            for q in range(4):
                op("dve", lambda q=q: nc.vector.tensor_copy(out=tmp[:], in_=Ytm[:, q * 4:(q + 1) * 4, :]), r=[ykey],
                   w=["dbgtmp"])
                dma("sp", dbg_ycat[l, m, q * 512:(q + 1) * 512, :].rearrange("(b p) n -> p b n", p=128), tmp[:],
                    r=["dbgtmp"], w=["dbgout"])

        def ssd(l):
            R0 = l * NROW
            alog_bc = rowsb[:, R0:R0 + 8]
            dtb_bc = rowsb[:, R0 + 8:R0 + 16]
            dsk_bc = rowsb[:, R0 + 16:R0 + 20]
            snw_bc = rowsb[:, R0 + 20:R0 + 276]

            def ph(bank, half):
                return ("ps", bank)

            with scope() as sc:
                Gc = sb("Gc", [128, NB, 256], BF16, sc)
                dtt = sb("dtt", [128, NB, 8], F32, sc)
                att = sb("att", [128, 2, NB, 4], F32, sc)
                W4 = sb("W4", [128, NB, 16], F32, sc)
                smallc = sb("smallc", [128, 384], F32, sc)
                nA = sb("nA", [128, 8], F32, sc)
                Xtm = sb("Xtm", [128, NB, 256], BF16, sc)
                Btm = sb("Btm", [128, NB, 256], BF16, sc)
                BCT = sb("BCT", [128, 4, T], BF16, sc)
                with scope() as spj:
                    wt = Wt
                    pre = [sb("pre%d" % i, [128, 2052], BF16, spj) for i in range(2)]
                    xsT = [sb("xsT%d" % i, [128, T], BF16, spj) for i in range(2)]
                    diagw = sb("diagw", [128, 30, 128], BF16, spj)
                    cwcol = sb("cwcol", [128, 60], F32, spj)
                    cbcol = sb("cbcol", [128, 12], F32, spj)
                    dma("sp", cwcol[:], cwcol_d, w=["cwcol"])
                    dma("sp", cbcol[:], cbcol_d, w=["cbcol"])
                    for k in range(2):
                        op("pool", lambda k=k: nc.gpsimd.memset(pre[k][:], 0.0), w=[("pre", k)])
                    for jc in range(30):
                        op("dve", lambda jc=jc: nc.vector.tensor_scalar(
                            out=diagw[:, jc, :], in0=identb[:], scalar1=cwcol[:, l * 30 + jc:l * 30 + jc + 1],
                            scalar2=None, op0=ALU.mult), r=["identb", "cwcol"], w=["diagw"])

                    def evac_zdt(ps, blk, bank):
                        op("act", lambda: nc.scalar.activation(out=Gc[:, blk, :], in_=ps[:, 0:256], func=AF.Silu),
                           r=[pk(bank)], w=["Gc"])
                        op("dve", lambda: nc.vector.tensor_tensor(out=dtt[:, blk, :], in0=ps[:, 256:264], in1=dtb_bc,
                                                                  op=ALU.add), r=[pk(bank), "rowsb"], w=["dtt"])
                    proj_tm(wt, "W", 0, 264, evac_zdt)
                    op("act", lambda: nc.scalar.activation(out=dtt[:], in_=dtt[:], func=AF.Exp), r=["dtt"], w=["dtt"])
                    op("dve", lambda: nc.vector.tensor_scalar(out=dtt[:], in0=dtt[:], scalar1=1.0, scalar2=None,
                                                              op0=ALU.add), r=["dtt"], w=["dtt"])
                    op("act", lambda: nc.scalar.activation(out=dtt[:], in_=dtt[:], func=AF.Ln), r=["dtt"], w=["dtt"])
                    op("act", lambda: nc.scalar.activation(out=nA[:], in_=alog_bc, func=AF.Exp), r=["rowsb"], w=["nA"])
                    for d_ in range(2):
                        op("dve", lambda d_=d_: nc.vector.scalar_tensor_tensor(
                            out=att[:, d_, :, :], in0=dtt[:, :, d_ * 4:(d_ + 1) * 4], scalar=-1.0,
                            in1=nA[:, d_ * 4:(d_ + 1) * 4].unsqueeze(1).broadcast_to([128, NB, 4]),
                            op0=ALU.mult, op1=ALU.mult), r=["dtt", "nA"], w=["att"])

                    mg = mod_gen(1, spj, pbanks=(0, 1), cbanks=(2, 2)) if l == 0 else iter(())
                    proj_banks[0] = [3, 4, 5, 6] if l == 0 else [0, 1, 2, 3, 4, 5, 6]
                    for ch in range(6):
                        k = ch % 2
                        proj_fm(wt, "W", 264 + ch * 128,
                                lambda ps, tq, bank, k=k: copy_alt(pre[k][:, 2 + tq * 512: 2 + (tq + 1) * 512], ps[:, :],
                                                                   [pk(bank)], [("pre", k)]))
                        for tq in range(4):
                            next(mg, None)
                            bank = proj_banks[0][proj_rot[0] % len(proj_banks[0])]
                            proj_rot[0] += 1
                            for j in range(5):
                                op("pe", lambda j=j, bank=bank: nc.tensor.matmul(
                                    PS[bank][:, :], diagw[:, j * 6 + ch, :], pre[k][:, tq * 512 + j: tq * 512 + j + 512],
                                    start=(j == 0), stop=(j == 4), skip_group_check=True),
                                   r=["diagw", ("pre", k)], w=[pk(bank)], sig=(j == 4))
                            if ch < 2:
                                dst, dkey = xsT[ch][:, tq * 512:(tq + 1) * 512], ("xsT", ch)
                            else:
                                dst, dkey = BCT[:, ch - 2, tq * 512:(tq + 1) * 512], ("BCT", ch - 2)
                            op("act", lambda bank=bank, dst=dst: nc.scalar.activation(
                                out=dst, in_=PS[bank][:, :], func=AF.Silu, bias=cbcol[:, l * 6 + ch:l * 6 + ch + 1]),
                               r=[pk(bank), "cbcol"], w=[dkey])
                        if ch < 4:
                            srcT = (lambda t0, ch=ch: xsT[ch][:, t0:t0 + 128]) if ch < 2 else \
                                (lambda t0, ch=ch: BCT[:, ch - 2, t0:t0 + 128])
                            skey = ("xsT", ch) if ch < 2 else ("BCT", ch - 2)
                            dstT = Xtm if ch < 2 else Btm
                            cc = ch % 2
                            psb = PS[7][:].bitcast(BF16)
                            for g in range(4):
                                for t in range(4):
                                    blk = g * 4 + t
                                    op("pe", lambda t=t, blk=blk: nc.tensor.transpose(
                                        psb[:, t * 128:(t + 1) * 128], srcT(blk * 128), identb[:]),
                                       r=[skey, "identb"], w=[pk(7)], sig=(t == 3))
                                copy_alt(dstT[:, g * 4:(g + 1) * 4, cc * 128:(cc + 1) * 128],
                                         psb[:, 0:512].rearrange("p (t n) -> p t n", t=4), [pk(7)],
                                         ["Xtm" if ch < 2 else "Btm"])

                    for _ in mg:
                        pass
                proj_banks[0] = [0, 1, 2, 3, 4, 5, 6]
                issue_next(l, "C")
                Y = sb("Yc", [128, NB, 256], F32, sc)
                Ytm = sb("YtmC", [128, NB, 256], BF16, sc)
                attf = [att[:, d_, :, :].rearrange("p c h -> p (c h)") for d_ in range(2)]
                for ki, (U, d_) in enumerate(((U_gt, 0), (U_lt, 1), (U_le, 0), (U_ge, 1))):
                    op("pe", lambda U=U, d_=d_, ki=ki: nc.tensor.matmul(
                        PS[6][:, ki * 64:(ki + 1) * 64], U, attf[d_], start=(ki == 0), stop=True,
                        skip_group_check=True), r=["att", "cst"], w=[pk(6)])
                op("pe", lambda: nc.tensor.matmul(
                    PS[6][:, 256:384], onesf, att[:].rearrange("p d c h -> p (d c h)"), start=False, stop=True,
                    skip_group_check=True), r=["att", "cst"], w=[pk(6)])
                op("act", lambda: nc.scalar.activation(out=smallc[:], in_=PS[6][:, 0:384], func=AF.Exp),
                   r=[pk(6)], w=["smallc"])
                op("dve", lambda: nc.vector.tensor_copy(out=W4[:, :, 0:8], in_=dtt[:]), r=["dtt"], w=["W4"])
                for d_ in range(2):
                    op("dve", lambda d_=d_: nc.vector.tensor_tensor(
                        out=W4[:, :, 8 + d_ * 4:12 + d_ * 4], in0=dtt[:, :, d_ * 4:(d_ + 1) * 4],
                        in1=smallc[:, d_ * 64:(d_ + 1) * 64].rearrange("p (c h) -> p c h", h=4), op=ALU.mult),
                       r=["dtt", "smallc"], w=["W4"])

                with scope() as ssw:
                    aU = [sb("aU%d" % i, [128, 4, 128], F32, ssw) for i in range(4)]
                    eD = [sb("eD%d" % i, [128, 4, 128], F32, ssw) for i in range(4)]
                    CBm = [sb("CBm%d" % i, [128, 2, 128], F32, ssw) for i in range(4)]
                    MT = [sb("MT%d" % i, [128, 4, 128], BF16, ssw) for i in range(4)]
                    Xw = [sb("Xw%d" % i, [128, 2, 256], BF16, ssw) for i in range(4)]
                    tz = [sb("tz%d" % i, [128, 4, 64], F32, ssw) for i in range(2)]
                    Sd = [sb("Sst%d" % i, [128, 256], F32, ssw) for i in range(2)]
                    Sbfd = [sb("Sbf%d" % i, [128, 256], BF16, ssw) for i in range(2)]
                    for dr_ in range(2):
                        op("dve", lambda dr_=dr_: nc.vector.memset(Sd[dr_][:], 0.0), w=[("S", dr_)])
                        op("dve", lambda dr_=dr_: nc.vector.memset(Sbfd[dr_][:], 0.0), w=[("Sbf", dr_)])

                    def prep(ci, dr_):
                        U1 = U_gt if dr_ == 0 else U_lt
                        U2 = U_le if dr_ == 0 else U_ge
                        d4 = dr_ * 4
                        c = ci if dr_ == 0 else NB - 1 - ci
                        k = dr_ * 2 + ci % 2
                        bD, bC = dr_, 2 + dr_
                        tsl = slice(c * 128, (c + 1) * 128)
                        op("pool", lambda: nc.gpsimd.tensor_tensor(
                            out=aU[k][:], in0=U2.unsqueeze(1).broadcast_to([128, 4, 128]),
                            in1=att[:, dr_, c, :].unsqueeze(2).broadcast_to([128, 4, 128]), op=ALU.mult),
                           r=["att", "cst"], w=[("aU", k)])
                        op("pe", lambda: nc.tensor.matmul(PS[bD][:, :], U1, aU[k][:].rearrange("p h l -> p (h l)"),
                                                          start=True, stop=True, skip_group_check=True),
                           r=[("aU", k), "cst"], w=[pk(bD)])
                        op("act", lambda: nc.scalar.activation(out=eD[k][:].rearrange("p h l -> p (h l)"),
                                                               in_=PS[bD][:, :], func=AF.Exp),
                           r=[pk(bD)], w=[("eD", k)])
                        for g in range(2):
                            op("pe", lambda g=g: nc.tensor.matmul(PS[bC][:, g * 128:(g + 1) * 128], BCT[:, g, tsl],
                                                                  BCT[:, 2 + g, tsl], start=True, stop=True,
                                                                  skip_group_check=True),
                               r=[("BCT", g), ("BCT", 2 + g)], w=[pk(bC)], sig=(g == 1))
                        op("dve", lambda: nc.vector.tensor_tensor(
                            out=CBm[k][:], in0=PS[bC][:, 0:256].rearrange("p (g l) -> p g l", g=2),
                            in1=U2.unsqueeze(1).broadcast_to([128, 2, 128]), op=ALU.mult),
                           r=[pk(bC), "cst"], w=[("CBm", k)])
                        op("dve", lambda: nc.vector.tensor_tensor(
                            out=MT[k][:].rearrange("p (g e) l -> p g e l", g=2),
                            in0=eD[k][:].rearrange("p (g e) l -> p g e l", g=2),
                            in1=CBm[k][:].unsqueeze(2).broadcast_to([128, 2, 2, 128]), op=ALU.mult),
                           r=[("eD", k), ("CBm", k)], w=[("MT", k)])
                        op("pool", lambda: nc.gpsimd.tensor_tensor(
                            out=Xw[k][:].rearrange("p k (h d) -> p k h d", h=4),
                            in0=Xtm[:, c, :].rearrange("p (h d) -> p h d", h=4).unsqueeze(1).broadcast_to(
                                [128, 2, 4, 64]),
                            in1=W4[:, c, :].rearrange("p (k e h) -> p k e h", k=2, e=2)[:, :, dr_, :].unsqueeze(
                                3).broadcast_to([128, 2, 4, 64]), op=ALU.mult),
                           r=["Xtm", "W4"], w=[("Xw", k)])

                    def main(ci, dr_):
                        d4 = dr_ * 4
                        S, Sbf = Sd[dr_], Sbfd[dr_]
                        Sk, Sbk = ("S", dr_), ("Sbf", dr_)
                        c = ci if dr_ == 0 else NB - 1 - ci
                        k = dr_ * 2 + ci % 2
                        kt = dr_
                        bY, bS = 4 + dr_, 6 + dr_
                        tsl = slice(c * 128, (c + 1) * 128)
                        for h in range(4):
                            op("pe", lambda h=h: nc.tensor.matmul(PS[bY][:, h * 64:(h + 1) * 64], MT[k][:, h, :],
                                                                  Xw[k][:, 0, h * 64:(h + 1) * 64], start=True,
                                                                  stop=True, skip_group_check=True),
                               r=[("MT", k), ("Xw", k)], w=[pk(bY)], sig=False)
                        for h in range(4):
                            op("pe", lambda h=h: nc.tensor.matmul(PS[bY][:, 256 + h * 64:256 + (h + 1) * 64],
                                                                  BCT[:, 2 + h // 2, tsl], Sbf[:, h * 64:(h + 1) * 64],
                                                                  start=True, stop=True, skip_group_check=True),
                               r=[("BCT", 2), ("BCT", 3), Sbk], w=[pk(bY)], sig=(h == 3))
                        for g in range(2):
                            op("pe", lambda g=g: nc.tensor.matmul(PS[bS][:, g * 128:(g + 1) * 128],
                                                                  Btm[:, c, g * 128:(g + 1) * 128],
                                                                  Xw[k][:, 1, g * 128:(g + 1) * 128], start=True,
                                                                  stop=True, skip_group_check=True),
                               r=["Btm", ("Xw", k)], w=[pk(bS)], sig=(g == 1))
                        op("pool", lambda: nc.gpsimd.tensor_tensor(
                            out=S[:].rearrange("p (h d) -> p h d", h=4), in0=S[:].rearrange("p (h d) -> p h d", h=4),
                            in1=smallc[:, 256 + dr_ * 64 + c * 4:256 + dr_ * 64 + c * 4 + 4].unsqueeze(2).broadcast_to([128, 4, 64]), op=ALU.mult),
                           r=[Sk, "smallc"], w=[Sk])
                        op("dve", lambda: nc.vector.tensor_tensor(out=S[:], in0=S[:], in1=PS[bS][:, 0:256],
                                                                  op=ALU.add), r=[Sk, pk(bS)], w=[Sk])
                        op("act", lambda: nc.scalar.copy(out=Sbf[:], in_=S[:]), r=[Sk], w=[Sbk])
                        op("dve", lambda: nc.vector.tensor_tensor(
                            out=tz[kt][:], in0=PS[bY][:, 256:512].rearrange("p (h d) -> p h d", h=4),
                            in1=smallc[:, 128 + dr_ * 64 + c * 4:128 + dr_ * 64 + c * 4 + 4].unsqueeze(2).broadcast_to([128, 4, 64]), op=ALU.mult),
                           r=[pk(bY), "smallc"], w=[("tz", kt)])
                        if ci < NB // 2:
                            op("dve", lambda: nc.vector.tensor_tensor(
                                out=Y[:, c, :], in0=PS[bY][:, 0:256], in1=tz[kt][:].rearrange("p h d -> p (h d)"),
                                op=ALU.add), r=[pk(bY), ("tz", kt)], w=[("Y", c)])
                        else:
                            op("dve", lambda: nc.vector.tensor_tensor(
                                out=tz[kt][:].rearrange("p h d -> p (h d)"), in0=PS[bY][:, 0:256],
                                in1=tz[kt][:].rearrange("p h d -> p (h d)"), op=ALU.add),
                               r=[pk(bY), ("tz", kt)], w=[("tz", kt)])
                            op("pool", lambda: nc.gpsimd.tensor_tensor(
                                out=Y[:, c, :], in0=Y[:, c, :], in1=tz[kt][:].rearrange("p h d -> p (h d)"),
                                op=ALU.add), r=[("Y", c), ("tz", kt)], w=[("Y", c)])

                    prep(0, 0)
                    prep(0, 1)
                    for ci in range(NB):
                        if ci + 1 < NB:
                            prep(ci + 1, 0)
                            prep(ci + 1, 1)
                        main(ci, 0)
                        main(ci, 1)

                with scope() as sep:
                    t1 = [sb("ct1_%d" % i, [128, 4, 256], F32, sep) for i in range(2)]
                    sq = sb("csq", [128, 4, 256], F32, sep)
                    rs = sb("crs", [128, 8], F32, sep)
                    Yall = [("Y", c) for c in range(NB)]
                    for g4 in range(4):
                        k = g4 % 2
                        bs = slice(g4 * 4, g4 * 4 + 4)
                        op("pool", lambda: nc.gpsimd.tensor_tensor(
                            out=t1[k][:].rearrange("p b (h d) -> p (b h) d", h=4),
                            in0=Xtm[:, bs, :].rearrange("p b (h d) -> p (b h) d", h=4),
                            in1=dsk_bc.unsqueeze(1).broadcast_to([128, 4, 4]).rearrange("p b h -> p (b h)").unsqueeze(
                                2).broadcast_to([128, 16, 64]), op=ALU.mult) if False else nc.gpsimd.tensor_tensor(
                            out=t1[k][:].rearrange("p b (h d) -> p b h d", h=4),
                            in0=Xtm[:, bs, :].rearrange("p b (h d) -> p b h d", h=4),
                            in1=dsk_bc.unsqueeze(1).unsqueeze(3).broadcast_to([128, 4, 4, 64]), op=ALU.mult),
                           r=["Xtm", "rowsb"], w=[("ct1", k)])
                        op("pool", lambda: nc.gpsimd.tensor_tensor(out=t1[k][:], in0=t1[k][:], in1=Y[:, bs, :], op=ALU.add),
                           r=[("ct1", k)] + Yall, w=[("ct1", k)])
                        op("dve", lambda: nc.vector.tensor_tensor(out=t1[k][:], in0=t1[k][:], in1=Gc[:, bs, :],
                                                                  op=ALU.mult), r=[("ct1", k), "Gc"], w=[("ct1", k)])
                        op("dve", lambda: nc.vector.tensor_tensor(out=sq[:], in0=t1[k][:], in1=t1[k][:], op=ALU.mult),
                           r=[("ct1", k)], w=["csq"])
                        op("dve", lambda: nc.vector.tensor_reduce(out=rs[:], in_=sq[:].rearrange("p b (g n) -> p (b g) n", g=2),
                                                                  axis=AX.X, op=ALU.add), r=["csq"], w=["crs"])
                        rsqrt_col(rs[:], rs[:], 1.0 / 128, ["crs"], ["crs"])
                        op("dve", lambda: nc.vector.tensor_tensor(
                            out=t1[k][:].rearrange("p b (g n) -> p (b g) n", g=2),
                            in0=t1[k][:].rearrange("p b (g n) -> p (b g) n", g=2),
                            in1=rs[:].unsqueeze(2).broadcast_to([128, 8, 128]), op=ALU.mult),
                           r=[("ct1", k), "crs"], w=[("ct1", k)])
                        op("dve", lambda: nc.vector.tensor_tensor(
                            out=Ytm[:, bs, :], in0=t1[k][:], in1=snw_bc.unsqueeze(1).broadcast_to([128, 4, 256]),
                            op=ALU.mult), r=[("ct1", k), "rowsb"], w=["YtmC"])
                    dbg_dump(l, 2, Ytm, "YtmC", sep)
                    back_transpose(Ytm, "YtmC", 4)

        if dbg:
            with scope() as sd:
                tmp = sb("dbgh", [128, 2048], F32, sd)
                for c in range(8):
                    op("dve", lambda c=c: nc.vector.tensor_copy(out=tmp[:], in_=hT[:, c, :]), r=hT_all, w=["dbgh"])
                    dma("sp", dbg_h[:, c * T:(c + 1) * T], tmp[:], r=["dbgh"], w=["dbgout"])

        chk('norm0')
        for l in range(2):
            R0 = l * NROW
            alog_bc = rowsb[:, R0:R0 + 8]
            dtb_bc = rowsb[:, R0 + 8:R0 + 16]
            dsk_bc = rowsb[:, R0 + 16:R0 + 20]
            snw_bc = rowsb[:, R0 + 20:R0 + 276]
            dnw_bc = rowsb[:, R0 + 276:R0 + 532]
            lam_init = 0.8 - 0.6 * math.exp(-0.3 * l)

            for mixer in ("A", "B", "D"):
                with scope() as sm:
                    nk = 4 if mixer == "B" else 2
                    ncol = 1280 if mixer == "B" else 1024
                    wt = Wt
                    wkey = "W"
                    vcol = 512
                    QTt = sb("QT" + mixer, [128, 4, T], BF16, sm)
                    KTt = sb("KT" + mixer, [128, nk, T], BF16, sm)
                    Vt = sb("V" + mixer, [128, NB, 324], BF16, sm)
                    Gt = sb("G" + mixer, [128, NB, 256], BF16, sm)
                    Ytm = sb("Ytm" + mixer, [128, NB, 256], BF16, sm)
                    Pt = [sb("P%s%d" % (mixer, i), [128, 512], BF16, sm) for i in range(5)]
                    OTs = [sb("OTs%s%d" % (mixer, i), [65, 1024], F32, sm) for i in range(2)]
                    op("dve", lambda: nc.vector.memset(Vt[:], 1.0), w=["V"])
                    chk(mixer + str(l) + "w")
                    op("pool", lambda: nc.gpsimd.memset(QTt[:], 0.0), w=["QT"])

                    def evac_q(ps, tq, bank, cidx):
                        for hh in range(2):
                            copy_alt(QTt[hh * 64:(hh + 1) * 64, cidx * 2 + hh, tq * 512:(tq + 1) * 512],
                                     ps[hh * 64:(hh + 1) * 64, :], [pk(bank)], ["QT"])
                    for cidx in range(2):
                        proj_fm(wt, wkey, cidx * 128, lambda ps, tq, bank, cidx=cidx: evac_q(ps, tq, bank, cidx))
                    def evac_kb(ps, tq, bank, cidx):
                        op("dve", lambda: nc.vector.tensor_scalar(
                            out=KTt[:, cidx, tq * 512:(tq + 1) * 512], in0=ps[:, :], scalar1=kmask[0], scalar2=None,
                            op0=ALU.mult), r=[pk(bank), "cst"], w=["KT"])
                        op("act", lambda: nc.scalar.activation(
                            out=KTt[:, 2 + cidx, tq * 512:(tq + 1) * 512], in_=ps[:, :], func=AF.Copy, scale=kmask[1]),
                           r=[pk(bank), "cst"], w=["KT"])
                    for cidx in range(2):
                        if mixer == "B":
                            proj_fm(wt, wkey, 256 + cidx * 128, lambda ps, tq, bank, cidx=cidx: evac_kb(ps, tq, bank, cidx))
                        else:
                            proj_fm(wt, wkey, 256 + cidx * 128,
                                    lambda ps, tq, bank, cidx=cidx: copy_alt(KTt[:, cidx, tq * 512:(tq + 1) * 512],
                                                                             ps[:, :], [pk(bank)], ["KT"]))

                    chk(mixer + str(l) + "q")
                    import os
                    VAR = os.environ.get("KVAR", "")

                    def evac_vg(ps, blk, bank):
                        if VAR != "nov":
                          op("dve", lambda: nc.vector.tensor_copy(
                            out=Vt[:, blk, 0:260].rearrange("p (h d) -> p h d", d=65)[:, :, 0:64],
                            in_=ps[:, 0:256].rearrange("p (h d) -> p h d", d=64)), r=[pk(bank)], w=["V"])
                        if VAR != "nog":
                          op("act", lambda: nc.scalar.activation(out=Gt[:, blk, :], in_=ps[:, 256:512], func=AF.Silu),
                           r=[pk(bank)], w=["G"])
                    proj_tm(wt, wkey, vcol, 512, evac_vg)
                    if mixer == "B":
                        for blk in range(NB):
                            op("dve", lambda blk=blk: nc.vector.scalar_tensor_tensor(
                                out=Gt[:, blk, :], in0=Gt[:, blk, :], scalar=(1.0 - lam_init), in1=dnw_bc,
                                op0=ALU.mult, op1=ALU.mult), r=["G", "rowsb"], w=["G"])

                    rkeys = ["QT", "KT", "V"]
                    chk(mixer + str(l) + "p")
                    if mixer in ("A", "B"):
                        absd = sb("absd" + mixer, [128, TW], I16, sm)
                        tabs = [sb("tab%s%d" % (mixer, i), [128, TW], BF16, sm) for i in range(2)]
                        op("pool", lambda: nc.gpsimd.iota(absd[:], pattern=[[1, TW]], base=-C0, channel_multiplier=-1),
                           w=["absd"])
                        negd = tabs[1][:].bitcast(I16)
                        op("pool", lambda: nc.gpsimd.iota(negd, pattern=[[-1, TW]], base=C0, channel_multiplier=1),
                           w=[("tab", 1)])
                        op("dve", lambda: nc.vector.tensor_tensor(out=absd[:], in0=absd[:], in1=negd, op=ALU.max),
                           r=["absd", ("tab", 1)], w=["absd"])
                        if mixer == "A":
                            multA = sb("multA", [128, TW], BF16, sm)
                            for s_ in range(0, TW, 1984):
                                dma("pool", multA[:, s_:s_ + 1984], multA_d[:, s_:s_ + 1984], w=["multA"])
                    else:
                        EVM = [sb("EVM%d" % i, [128, 9 * 768], BF16, sm) for i in range(3)]

                        def load_evm(h_):
                            kk = h_ % 3
                            for q4 in range(4):
                                dma("pool", EVM[kk][:, q4 * 1728:(q4 + 1) * 1728],
                                    rpbm_d[l, h_, :, q4 * 1728:(q4 + 1) * 1728], w=[("EVM", kk)])
                        for h_ in range(3):
                            load_evm(h_)

                    chk(mixer + str(l) + "t")
                    otg = [sb("otg%s%d" % (mixer, g), [128, 4, 64], F32, sm) for g in range(2)]
                    rdg = [sb("rdg%s%d" % (mixer, g), [128, 4], F32, sm) for g in range(2)]
                    if mixer == "B":
                        O0 = sb("O0B", [128, NB, 64], F32, sm)
                        sqg = [sb("sqB%d" % g, [128, 4, 64], F32, sm) for g in range(2)]
                        rsg = [sb("rsB%d" % g, [128, 4], F32, sm) for g in range(2)]

                    def norm_group(psT, dst, g, dkey):
                        op("dve", lambda: nc.vector.reciprocal(out=rdg[g][:].unsqueeze(2), in_=psT[:, :, 64:65]),
                           r=[pk(7)], w=[("rdg", g)])
                        op("dve", lambda: nc.vector.tensor_tensor(out=dst, in0=psT[:, :, 0:64],
                                                                  in1=rdg[g][:].unsqueeze(2).broadcast_to([128, 4, 64]),
                                                                  op=ALU.mult), r=[pk(7), ("rdg", g)], w=[dkey])

                    def handlers(psT, blk0, g, h, i):
                        ok = ("otg", g)
                        ysl = Ytm[:, blk0:blk0 + 4, h * 64:(h + 1) * 64]
                        gsl = Gt[:, blk0:blk0 + 4, h * 64:(h + 1) * 64]
                        if mixer != "B":
                            def da():
                                norm_group(psT, otg[g][:], g, ok)
                                op("dve", lambda: nc.vector.tensor_tensor(out=ysl, in0=otg[g][:], in1=gsl, op=ALU.mult),
                                   r=[ok, "G"], w=["Ytm"])
                            return [da]
                        if i == 0:
                            def da():
                                norm_group(psT, O0[:, blk0:blk0 + 4, :], g, ("O0", blk0))
                            return [da]

                        def da():
                            norm_group(psT, otg[g][:], g, ok)
                            op("dve", lambda: nc.vector.scalar_tensor_tensor(
                                out=otg[g][:], in0=otg[g][:], scalar=nlam[:, l:l + 1], in1=O0[:, blk0:blk0 + 4, :],
                                op0=ALU.mult, op1=ALU.add), r=[ok, "nlam", ("O0", blk0)], w=[ok])
                            op("dve", lambda: nc.vector.tensor_tensor(out=sqg[g][:], in0=otg[g][:], in1=otg[g][:],
                                                                      op=ALU.mult), r=[ok], w=[("sqg", g)])
                            op("dve", lambda: nc.vector.tensor_reduce(out=rsg[g][:], in_=sqg[g][:], axis=AX.X, op=ALU.add),
                               r=[("sqg", g)], w=[("rsg", g)])
                            op("dve", lambda: nc.vector.tensor_scalar(out=rsg[g][:], in0=rsg[g][:], scalar1=1.0 / 64,
                                                                      scalar2=EPS, op0=ALU.mult, op1=ALU.add),
                               r=[("rsg", g)], w=[("rsg", g)])

                        def aa():
                            op("act", lambda: nc.scalar.activation(out=rsg[g][:], in_=rsg[g][:], func=AF.Ln),
                               r=[("rsg", g)], w=[("rsg", g)])
                            op("act", lambda: nc.scalar.activation(out=rsg[g][:], in_=rsg[g][:], func=AF.Exp, scale=-0.5),
                               r=[("rsg", g)], w=[("rsg", g)])

                        def db():
                            op("dve", lambda: nc.vector.tensor_tensor(
                                out=otg[g][:], in0=otg[g][:], in1=rsg[g][:].unsqueeze(2).broadcast_to([128, 4, 64]),
                                op=ALU.mult), r=[ok, ("rsg", g)], w=[ok])
                            op("dve", lambda: nc.vector.tensor_tensor(out=ysl, in0=otg[g][:], in1=gsl, op=ALU.mult),
                               r=[ok, "G"], w=["Ytm"])
                        return [da, aa, db]

                    epi = make_epilogue(OTs, handlers)

                    def bind_epi(h, i):
                        return lambda ob, qbase: epi(ob, qbase, h, i)

                    for h in range(4):
                        cidx, pb = h // 2, (h % 2) * 64
                        if mixer in ("A", "B"):
                            tb = tabs[h % 2]
                            tkey = ("tab", h % 2)
                            slope = slopes_a[h] if mixer == "A" else slopes_b[h]
                            if mixer == "A":
                                op("act", lambda: nc.scalar.activation(out=tb[:], in_=absd[:], func=AF.Exp, scale=-slope),
                                   r=["absd"], w=[tkey])
                                op("pool", lambda: nc.gpsimd.tensor_tensor(out=tb[:], in0=tb[:], in1=multA[:], op=ALU.mult),
                                   r=[tkey, "multA"], w=[tkey])
                            else:
                                op("act", lambda: nc.scalar.activation(out=tb[:], in_=absd[:], func=AF.Exp, scale=-slope),
                                   r=["absd"], w=[tkey])
                            mults = [("dve", lambda j, qa, qb, tb=tb: tb[:, qa - j * 128 + C0: qb - j * 128 + C0], [tkey])]
                        else:
                            k2 = h % 3
                            if h == 1:
                                load_evm(3)
                            op("act", lambda: nc.scalar.activation(out=EVM[k2][:], in_=EVM[k2][:], func=AF.Exp),
                               r=[("EVM", k2)], w=[("EVM", k2)])

                            def evf(j, qa, qb, k2=k2):
                                o = d_cls(j) * 768 + qa - d_ilo(j) * 128
                                return EVM[k2][:, o:o + (qb - qa)]
                            mults = [("dve", evf, [("EVM", k2)])]
                        if h == 0:
                            issue_next(l, mixer)
                        nsub = 2 if mixer == "B" else 1
                        for half in range(2):
                            for i in range(nsub):
                                if mixer == "B":
                                    KTf = lambda j, cidx=cidx, pb=pb, i=i: KTt[:, i * 2 + cidx, j * 128:(j + 1) * 128]
                                    sc_ = 32 ** -0.5
                                else:
                                    KTf = lambda j, cidx=cidx, pb=pb: KTt[:, cidx, j * 128:(j + 1) * 128]
                                    sc_ = 64 ** -0.5
                                QTf = lambda qa, qb, h=h: QTt[:, h, qa:qb]
                                Vf = lambda j, h=h: Vt[:, j, h * 65:h * 65 + 128]
                                steps = []
                                q0, q1 = half * 1024, (half + 1) * 1024
                                for j in range(NB):
                                    if mixer in ("A", "B"):
                                        kmax = int(math.floor(64.0 / slope / 128.0 - 1e-9)) + 1
                                        if mixer == "A":
                                            kmax = min(kmax, 8)
                                        lo, hi = max(0, (j - kmax) * 128), min(T, (j + kmax + 1) * 128)
                                    else:
                                        lo, hi = d_ilo(j) * 128, (d_ihi(j) + 1) * 128
                                    lo, hi = max(lo, q0), min(hi, q1)
                                    a = lo
                                    while a < hi:
                                        bnd = min(hi, (a // 512 + 1) * 512)
                                        steps.append((j, a, bnd))
                                        a = bnd
                                attention_task(KTf, QTf, Vf, steps, sc_, mults, rkeys, bind_epi(h, i), Pt)
                                chk(mixer + str(l) + "k")
                    while pending:
                        run_pending(1)
                    mi = {"A": 0, "B": 1, "D": 3}[mixer]
                    dbg_dump(l, mi, Ytm, "Ytm", sm)
                    back_transpose(Ytm, "Ytm", mi * 2)
                chk(mixer + str(l))

                if mixer == "B":
                    ssd(l)
                    chk('C' + str(l))

            with scope() as so:
                woutg = sb("woutg", [128, 8, D], BF16, so)
                wst = [sb("wst%d" % i, [128, D], F32, so) for i in range(4)]
                for c in range(8):
                    dma("sp", wst[c % 4][:], wout_d[l, c * 128:(c + 1) * 128, :], w=[("wst", c % 4)])
                    if c % 2 == 0:
                        op("pool", lambda c=c: nc.gpsimd.tensor_tensor(out=woutg[:, c, :], in0=wst[c % 4][:],
                                                                       in1=gatebc[:, l, :], op=ALU.mult),
                           r=[("wst", c % 4), "gatebc"], w=[("woutg", 0)])
                    else:
                        op("dve", lambda c=c: nc.vector.tensor_tensor(out=woutg[:, c, :], in0=wst[c % 4][:],
                                                                      in1=gatebc[:, l, :], op=ALU.mult),
                           r=[("wst", c % 4), "gatebc"], w=[("woutg", 1)])
                xo = [sb("xo%d" % i, [128, D], F32, so) for i in range(4)]
                junk = [sb("junk%d" % i, [128, D], BF16, so) for i in range(2)]
                ssq = sb("ssq", [128, 4], F32, so)
                xn = [sb("xn%d" % i, [128, D], BF16, so) for i in range(8)]
                if l == 1:
                    fnw = sb("fnwbc", [128, D], F32, so)
                    dma("sp", fnw[:], fnw_d.partition_broadcast(128), w=["fnw"])
                    ot = [sb("ot%d" % i, [128, D], F32, so) for i in range(2)]
                src = x_d if l == 0 else x1_d

                def xload(blk):
                    dma("sp", xo[blk % 4][:], src[blk * 128:(blk + 1) * 128, :], r=["x1s"], w=[("xo", blk % 4)])

                def stage_a(blk):
                    k4 = blk % 4
                    if blk + 2 < NB:
                        xload(blk + 2)
                    for n in range(2):
                        bank = (blk * 2 + n) % 4
                        for c in range(8):
                            op("pe", lambda c=c, n=n, bank=bank: nc.tensor.matmul(
                                PS[bank][:, :], ycatT[:, c, blk * 128:(blk + 1) * 128], woutg[:, c, n * 512:(n + 1) * 512],
                                start=(c == 0), stop=(c == 7), skip_group_check=True),
                               r=[("ycatT", c, blk // 4), ("woutg", 0), ("woutg", 1)], w=[pk(bank)], sig=(c == 7))
                        op("dve", lambda n=n, bank=bank: nc.vector.tensor_tensor(
                            out=xo[k4][:, n * 512:(n + 1) * 512], in0=PS[bank][:, :], in1=xo[k4][:, n * 512:(n + 1) * 512],
                            op=ALU.add), r=[pk(bank), ("xo", k4)], w=[("xo", k4)])
                    if l == 0:
                        dma("sp", x1_d[blk * 128:(blk + 1) * 128, :], xo[k4][:], r=[("xo", k4)], w=["x1s"])

                def stage_b(blk):
                    k4 = blk % 4
                    if l == 0:
                        k8 = blk % 8
                        norm_pre(xo[k4][:], ("xo", k4), ssq[:, k4:k4 + 1], ("ssq", k4), xn[k8][:], ("xn", k8),
                                 junk[blk % 2][:], ("junk", blk % 2))
                        if blk % 4 == 3:
                            b0 = blk - 3
                            norm_post4(1, [xn[(b0 + t) % 8] for t in range(4)], [("xn", (b0 + t) % 8) for t in range(4)],
                                       b0, [4, 5, 6, 7])
                    else:
                        k = blk % 2
                        op("act", lambda: nc.scalar.activation(out=junk[k][:], in_=xo[k4][:], func=AF.Square,
                                                               accum_out=ssq[:, k:k + 1]),
                           r=[("xo", k4)], w=[("junk", k), ("ssq", k)])
                        rsqrt_col(ssq[:, k:k + 1], ssq[:, k:k + 1], 1.0 / D, [("ssq", k)], [("ssq", k)])
                        op("dve", lambda: nc.vector.scalar_tensor_tensor(
                            out=ot[k][:], in0=xo[k4][:], scalar=ssq[:, k:k + 1], in1=fnw[:], op0=ALU.mult, op1=ALU.mult),
                           r=[("xo", k4), ("ssq", k), "fnw"], w=[("ot", k)])
                        dma("sp", out_d[blk * 128:(blk + 1) * 128, :], ot[k][:], r=[("ot", k)], w=["outd"])

                xload(0)
                xload(1)
                stage_a(0)
                for blk in range(NB):
                    if blk + 1 < NB:
                        stage_a(blk + 1)
                    stage_b(blk)


def d_ilo(j):
    if j == 3:
        return 0
    return max(0, j - 2)


def d_cls(j):
    if 4 <= j <= 11:
        return 0
    return j + 1 if j < 4 else j - 7


def d_ihi(j):
    if j == 12:
        return 15
    return min(15, j + 2)


def _consts():
    m = np.arange(128)[:, None]
    n = np.arange(128)[None, :]
    ka = ((np.arange(128) % 64) < 32).astype(np.float64)[:, None]
    cst = np.concatenate([np.eye(128), (m <= n), (m > n), (m < n), (m >= n), np.ones((128, 128)), ka, 1.0 - ka], axis=1)
    cst = cst.astype(np.float32)
    d = (np.arange(TW)[None, :] - C0 - np.arange(128)[:, None])
    ad = np.abs(d)
    mult = (ad <= 64).astype(np.float32) + ((d % 4 == 0) & (ad <= 256)) + ((d % 16 == 0) & (ad <= 1024))
    multA = mult.astype(np.float32)
    kr = np.arange(128)
    b_, ck = kr // 64, kr % 64
    qr = np.arange(128)
    a_, cq = qr // 64, qr % 64
    cs = np.clip(cq - 8, 0, 48)
    colok = (ck[:, None] >= cs[None, :]) & (ck[:, None] <= cs[None, :] + 15)
    drr = np.zeros((128, 9, 6, 128), np.int64)
    dcc = np.zeros((128, 9, 6, 128), np.int64)
    val = np.zeros((128, 9, 6, 128), bool)
    reps = {0: 6}
    for j in range(16):
        reps.setdefault(d_cls(j), j)
    for j in sorted(reps.values()) + list(range(16)):
        cls = d_cls(j)
        for i in range(16):
            rq = 2 * i + a_
            rk = 2 * j + b_
            rs = np.clip(rq - 4, 0, 24)
            rowok = (rk[:, None] >= rs[None, :]) & (rk[:, None] <= rs[None, :] + 7)
            v = rowok & colok
            if d_ilo(j) <= i <= d_ihi(j):
                e = i - d_ilo(j)
                dr_ = np.clip(rk[:, None] - rq[None, :] + 7, 0, 14)
                dc_ = np.clip(ck[:, None] - cq[None, :] + 15, 0, 30)
                if reps[cls] == j:
                    drr[:, cls, e, :] = dr_
                    dcc[:, cls, e, :] = dc_
                    val[:, cls, e, :] = v
                else:
                    assert (val[:, cls, e, :] == v).all() and (drr[:, cls, e, :][v] == dr_[v]).all()
            else:
                assert not v.any(), (i, j)
    return cst, multA, drr, dcc, val


_CACHE = {}
_STOP = None
_NCORES = 8


def kernel(x, c, norm_w, ada_w, ada_b, w_in, diff_lambda, diff_norm_w, conv_w, conv_b, ssm_a_log, ssm_dt_bias,
           ssm_d, ssm_norm_w, na_rpb, w_out, final_norm_w, _dbg=False):
    f = lambda a: np.ascontiguousarray(np.asarray(a, dtype=np.float32))
    x, c, norm_w, ada_w, ada_b, w_in = f(x), f(c), f(norm_w), f(ada_w), f(ada_b), f(w_in)
    cst, multA, drr, dcc, val = _consts()
    rpb = f(na_rpb)
    rpbm = f(np.where(val[None, None], rpb[:, :, drr, dcc], np.float32(-30000.0)).reshape(2, 4, 128, 9 * 768))
    rows = []
    for l in range(2):
        rows.append(np.concatenate([f(ssm_a_log)[l].reshape(-1), f(ssm_dt_bias)[l].reshape(-1), f(ssm_d)[l].reshape(-1),
                                    f(ssm_norm_w)[l].reshape(-1), np.tile(f(diff_norm_w)[l].reshape(-1), 4)]))
    rows = f(np.concatenate(rows).reshape(1, -1))
    nw_col = f(norm_w.reshape(2, 8, 128).transpose(2, 0, 1).reshape(128, 16))
    cw_col = f(f(conv_w).reshape(2, 5, 6, 128).transpose(3, 0, 1, 2).reshape(128, 60))
    cb_col = f(f(conv_b).reshape(2, 6, 128).transpose(2, 0, 1).reshape(128, 12))
    ada_w = f(ada_w.reshape(2, 8, 128, 24, 128).transpose(0, 3, 2, 1, 4).reshape(2, 24, 128, 1024))
    shared = {"cst": cst, "ada_w": ada_w, "ada_b": f(ada_b.reshape(1, -1)), "nw_col": nw_col, "w_in": w_in,
              "w_out": f(w_out), "fnw": f(final_norm_w).reshape(1, -1), "rows": rows,
              "dlam": f(diff_lambda).reshape(1, -1), "cw_col": cw_col, "cb_col": cb_col, "multA": multA,
              "rpbm": rpbm}
    in_maps = []
    for b in range(_NCORES):
        m = dict(shared)
        m["x"] = x[b]
        m["ccol"] = f(c[b].reshape(8, 128).T)
        in_maps.append(m)
    key = bool(_dbg)
    if key not in _CACHE:
        _CACHE[key] = build_program(dbg=_dbg, stop=_STOP)
    nc = _CACHE[key]
    res = run_bass_kernel_spmd(nc, in_maps, core_ids=list(range(_NCORES)))
    out = np.stack([np.asarray(r["out"], dtype=np.float32) for r in res.results], axis=0)
    if _dbg:
        return out, res.results
    return out
```
